# Optimizing a Trainium2 kernel written in Bass

```python
import math
import jax, jax.numpy as jnp
from jax import lax
import numpy as np

D_MODEL = 2048
BATCH = 4
SEQ = 2048
DEPTH = 4

N_MIXERS = 3
D_INNER = D_MODEL
S5_GROUP = 16
S5_STATE = 64
S5_GROUPS = D_INNER // S5_GROUP
FOX_HEAD_DIM = 128
FOX_HEADS = D_INNER // FOX_HEAD_DIM
Q_BLOCK = 128
POOL_WINDOWS = (2, 4, 8, 16)
POOL_GROUPS = len(POOL_WINDOWS)
POOL_GROUP_DIM = D_INNER // POOL_GROUPS
N_S5 = (DEPTH + 2) // 3
N_FOX = (DEPTH + 1) // 3
N_POOL = DEPTH // 3
EPS = 1e-6
DT_MIN = 1e-3
DT_MAX = 1e-1

kernel_name = "hybrid_s5_fox_pool_interleaved"


def rmsnorm(x, w):
    xf = x.astype(jnp.float32)
    y = xf * lax.rsqrt(jnp.mean(xf * xf, axis=-1, keepdims=True) + EPS)
    return (y * w.astype(jnp.float32)).astype(x.dtype)


def _s5_combine(left, right):
    a1r, a1i, b1r, b1i = left
    a2r, a2i, b2r, b2i = right
    ar = a2r * a1r - a2i * a1i
    ai = a2r * a1i + a2i * a1r
    br = a2r * b1r - a2i * b1i + b2r
    bi = a2r * b1i + a2i * b1r + b2i
    return (ar, ai, br, bi)


def s5_mixer(u, a_re, a_im, log_dt, b_re, b_im, c_re, c_im, d_skip, w_glu, b_glu):
    f32 = jnp.float32
    bsz, L, E = u.shape
    ar = a_re.astype(f32)
    ai = a_im.astype(f32)
    dt = jnp.exp(log_dt.astype(f32))[:, None]
    mag = jnp.exp(ar * dt)
    abar_r = mag * jnp.cos(ai * dt)
    abar_i = mag * jnp.sin(ai * dt)
    den = ar * ar + ai * ai
    xr = abar_r - 1.0
    fr = (xr * ar + abar_i * ai) / den
    fi = (abar_i * ar - xr * ai) / den
    br = b_re.astype(f32)
    bi = b_im.astype(f32)
    bbar_r = fr[..., None] * br - fi[..., None] * bi
    bbar_i = fr[..., None] * bi + fi[..., None] * br
    ug = u.astype(f32).reshape(bsz, L, S5_GROUPS, S5_GROUP)
    bu_r = jnp.einsum('blgc,gpc->blgp', ug, bbar_r)
    bu_i = jnp.einsum('blgc,gpc->blgp', ug, bbar_i)
    a_r_el = jnp.broadcast_to(abar_r[None, None], (1, L, S5_GROUPS, S5_STATE))
    a_i_el = jnp.broadcast_to(abar_i[None, None], (1, L, S5_GROUPS, S5_STATE))
    _, _, h_r, h_i = lax.associative_scan(_s5_combine, (a_r_el, a_i_el, bu_r, bu_i), axis=1)
    y = (jnp.einsum('blgp,gcp->blgc', h_r, c_re.astype(f32))
         - jnp.einsum('blgp,gcp->blgc', h_i, c_im.astype(f32)))
    y = y.reshape(bsz, L, E) + d_skip.astype(f32) * u.astype(f32)
    g = jax.nn.gelu(y)
    y = g * jax.nn.sigmoid(g @ w_glu.astype(f32) + b_glu.astype(f32))
    return y.astype(u.dtype)


def fox_mixer(q, k, v, f_logit, q_norm_w, k_norm_w):
    f32 = jnp.float32
    bsz, L, H, Dh = q.shape
    q = rmsnorm(q, q_norm_w)
    k = rmsnorm(k, k_norm_w)
    cum = jnp.cumsum(jax.nn.log_sigmoid(f_logit.astype(f32)), axis=1)
    cum_k = cum.transpose(0, 2, 1)[:, :, None, :]
    scale = Dh ** -0.5
    nb = L // Q_BLOCK
    qb = q.reshape(bsz, nb, Q_BLOCK, H, Dh).transpose(1, 0, 2, 3, 4)
    cb = cum.reshape(bsz, nb, Q_BLOCK, H).transpose(1, 0, 2, 3)
    kpos = jnp.arange(L)

    def block(args):
        i, q_i, c_i = args
        s = jnp.einsum('bqhd,bkhd->bhqk', q_i, k).astype(f32) * scale
        s = s + c_i.transpose(0, 2, 1)[..., None] - cum_k
        qpos = i * Q_BLOCK + jnp.arange(Q_BLOCK)
        s = jnp.where(kpos[None, :] <= qpos[:, None], s, -jnp.inf)
        p = jax.nn.softmax(s, axis=-1)
        return jnp.einsum('bhqk,bkhd->bqhd', p.astype(v.dtype), v)

    out = lax.map(block, (jnp.arange(nb), qb, cb))
    return out.transpose(1, 0, 2, 3, 4).reshape(bsz, L, H * Dh)


def pool_mixer(u, w_group, layer_scale):
    f32 = jnp.float32
    bsz, L, E = u.shape
    uf = u.astype(f32)
    cs = jnp.concatenate([jnp.zeros((bsz, 1, E), f32), jnp.cumsum(uf, axis=1)], axis=1)
    t = jnp.arange(L)
    pooled = []
    for g, w in enumerate(POOL_WINDOWS):
        sl = slice(g * POOL_GROUP_DIM, (g + 1) * POOL_GROUP_DIM)
        lo = jnp.maximum(t + 1 - w, 0)
        s = cs[:, 1:, sl] - cs[:, lo, sl]
        cnt = jnp.minimum(t + 1, w).astype(f32)
        pooled.append(s / cnt[None, :, None])
    pooled = jnp.stack(pooled, axis=2)
    ug = uf.reshape(bsz, L, POOL_GROUPS, POOL_GROUP_DIM)
    mixed = jnp.einsum('blgc,gcd->blgd', pooled - ug, w_group.astype(f32))
    return (mixed.reshape(bsz, L, E) * layer_scale.astype(f32)).astype(u.dtype)


def setup_inputs(seed: int = 0) -> dict:
    key = jax.random.key(seed)
    ks = jax.random.split(key, 24)
    f32 = jnp.float32
    nrm = lambda k, shape, s: jax.random.normal(k, shape, f32) * s
    E, G, P, C = D_INNER, S5_GROUPS, S5_STATE, S5_GROUP
    x = nrm(ks[0], (BATCH, SEQ, D_MODEL), 1.0)
    norm_w = 1.0 + nrm(ks[1], (DEPTH, D_MODEL), 0.02)
    out_proj = nrm(ks[2], (DEPTH, E, D_MODEL), E ** -0.5 / math.sqrt(DEPTH))
    s5_in_proj = nrm(ks[3], (N_S5, D_MODEL, 2 * E), D_MODEL ** -0.5)
    n_idx = jnp.arange(P, dtype=f32)
    s5_a_re = -0.5 + nrm(ks[4], (N_S5, G, P), 0.01)
    s5_a_im = math.pi * n_idx[None, None, :] + nrm(ks[5], (N_S5, G, P), 0.01)
    s5_log_dt = jax.random.uniform(ks[6], (N_S5, G), f32, math.log(DT_MIN), math.log(DT_MAX))
    s5_b_re = nrm(ks[7], (N_S5, G, P, C), (2 * C) ** -0.5)
    s5_b_im = nrm(ks[8], (N_S5, G, P, C), (2 * C) ** -0.5)
    s5_c_re = nrm(ks[9], (N_S5, G, C, P), (2 * P) ** -0.5 * 2.0)
    s5_c_im = nrm(ks[10], (N_S5, G, C, P), (2 * P) ** -0.5 * 2.0)
    s5_d = nrm(ks[11], (N_S5, E), 1.0)
    s5_w_glu = nrm(ks[12], (N_S5, E, E), E ** -0.5)
    s5_b_glu = nrm(ks[13], (N_S5, E), 0.01)
    fox_in_proj = nrm(ks[14], (N_FOX, D_MODEL, 4 * E + FOX_HEADS), D_MODEL ** -0.5)
    fox_q_norm = 1.0 + nrm(ks[15], (N_FOX, FOX_HEAD_DIM), 0.02)
    fox_k_norm = 1.0 + nrm(ks[16], (N_FOX, FOX_HEAD_DIM), 0.02)
    fox_f_bias = jax.random.uniform(ks[17], (N_FOX, FOX_HEADS), f32, 1.0, 4.0)
    pool_in_proj = nrm(ks[18], (N_POOL, D_MODEL, 2 * E), D_MODEL ** -0.5)
    pool_w_group = nrm(ks[19], (N_POOL, POOL_GROUPS, POOL_GROUP_DIM, POOL_GROUP_DIM), POOL_GROUP_DIM ** -0.5)
    pool_scale = 1.0 + nrm(ks[20], (N_POOL, E), 0.1)
    return {"x": x, "norm_w": norm_w, "out_proj": out_proj,
            "s5_in_proj": s5_in_proj, "s5_a_re": s5_a_re, "s5_a_im": s5_a_im,
            "s5_log_dt": s5_log_dt, "s5_b_re": s5_b_re, "s5_b_im": s5_b_im,
            "s5_c_re": s5_c_re, "s5_c_im": s5_c_im, "s5_d": s5_d,
            "s5_w_glu": s5_w_glu, "s5_b_glu": s5_b_glu,
            "fox_in_proj": fox_in_proj, "fox_q_norm": fox_q_norm,
            "fox_k_norm": fox_k_norm, "fox_f_bias": fox_f_bias,
            "pool_in_proj": pool_in_proj, "pool_w_group": pool_w_group,
            "pool_scale": pool_scale}


def reference(x, norm_w, out_proj, s5_in_proj, s5_a_re, s5_a_im, s5_log_dt, s5_b_re, s5_b_im,
              s5_c_re, s5_c_im, s5_d, s5_w_glu, s5_b_glu, fox_in_proj, fox_q_norm, fox_k_norm,
              fox_f_bias, pool_in_proj, pool_w_group, pool_scale):
    E = D_INNER
    bsz, L, _ = x.shape
    h = x
    for i in range(DEPTH):
        kind = i % N_MIXERS
        j = i // N_MIXERS
        xn = rmsnorm(h, norm_w[i])
        if kind == 0:
            proj = xn @ s5_in_proj[j]
            u, z = proj[..., :E], proj[..., E:]
            y = s5_mixer(u, s5_a_re[j], s5_a_im[j], s5_log_dt[j], s5_b_re[j], s5_b_im[j],
                         s5_c_re[j], s5_c_im[j], s5_d[j], s5_w_glu[j], s5_b_glu[j])
        elif kind == 1:
            proj = xn @ fox_in_proj[j]
            hs = (bsz, L, FOX_HEADS, FOX_HEAD_DIM)
            q = proj[..., :E].reshape(hs)
            k = proj[..., E:2 * E].reshape(hs)
            v = proj[..., 2 * E:3 * E].reshape(hs)
            z = proj[..., 3 * E:4 * E]
            f_logit = proj[..., 4 * E:] + fox_f_bias[j]
            y = fox_mixer(q, k, v, f_logit, fox_q_norm[j], fox_k_norm[j])
        else:
            proj = xn @ pool_in_proj[j]
            u, z = proj[..., :E], proj[..., E:]
            y = pool_mixer(u, pool_w_group[j], pool_scale[j])
        h = h + (y * jax.nn.silu(z)) @ out_proj[i]
    return h
```

```python
import contextlib
import math
import os
import numpy as np
import concourse.bass as bass
import concourse.mybir as mybir
from concourse.bass_utils import run_bass_kernel_spmd

F32 = mybir.dt.float32
BF16 = mybir.dt.bfloat16
I32 = mybir.dt.int32
ALU = mybir.AluOpType
AF = mybir.ActivationFunctionType
AX = mybir.AxisListType

D = 2048
E = 2048
DEPTH = 4
EPS = 1e-6
NCORES = 8
POOL_WINDOWS = (2, 4, 8, 16)


class Buf:
    __slots__ = ("t", "w", "r", "name", "uid")
    _n = [0]

    def __init__(self, t, name=""):
        self.t = t
        self.w = None
        self.r = {}
        self.name = name
        Buf._n[0] += 1
        self.uid = Buf._n[0]

    def __getitem__(self, k):
        return self.t[k]


class Prog:
    ENGS = ("sync", "scalar", "vector", "gpsimd", "tensor")

    def __init__(self, nc, st):
        self.nc = nc
        self.st = st
        self.sem = {}
        self.cnt = {}
        for e in self.ENGS:
            self.sem[e] = st.enter_context(nc.semaphore("s_" + e))
            self.cnt[e] = 0
        self.last = {e: None for e in self.ENGS}
        self.seen = {e: {} for e in self.ENGS}
        self.NDS = 96
        self.dkeys = ["dsem%d" % i for i in range(self.NDS)]
        for k in self.dkeys:
            self.sem[k] = st.enter_context(nc.semaphore(k))
            self.cnt[k] = 0
        self.dassign = {}
        self.dfree = {"sw": [k for i, k in enumerate(self.dkeys) if i < 40],
                      "hw": [k for i, k in enumerate(self.dkeys) if i >= 40]}

    def eng(self, e):
        return getattr(self.nc, e)

    def tick(self, e):
        if self.last[e] is None:
            return (e, self.cnt[e])
        self.cnt[e] += 1
        self.last[e].then_inc(self.sem[e], 1)
        self.last[e] = None
        return (e, self.cnt[e])

    def wait(self, e, tok):
        if tok is None:
            return
        key, val = tok
        if val <= 0 or self.seen[e].get(key, 0) >= val:
            return
        if key == e and self.last[e] is not None and self.cnt[e] < val:
            raise RuntimeError("waiting on own future")
        self.eng(e).wait_ge(self.sem[key], val)
        self.seen[e][key] = val

    def _pre(self, e, reads, writes):
        for b in reads:
            self.wait(e, b.w)
        for b in writes:
            self.wait(e, b.w)
            for tok in b.r.values():
                self.wait(e, tok)

    def _post(self, tok, reads, writes):
        for b in writes:
            b.w = tok
            b.r = {}
        for b in reads:
            b.r[tok[0]] = tok

    def op(self, e, fn, reads=(), writes=(), tick=True):
        self._pre(e, reads, writes)
        self.last[e] = fn(self.eng(e))
        if tick:
            tok = self.tick(e)
            self._post(tok, reads, writes)
            return tok
        return None

    def commit(self, e, reads=(), writes=()):
        tok = self.tick(e)
        self._post(tok, reads, writes)
        return tok

    def barrier(self, keep=()):
        kept = {k: v for k, v in self.dassign.items() if any(b is not None and b.uid == k[0] for b in keep)}
        toks = [self.tick(e) for e in self.ENGS]
        toks += [(k, v) for k, v in self.cnt.items() if k not in self.ENGS and k not in kept.values()]
        for e in self.ENGS:
            for t in toks:
                self.wait(e, t)
        self.dassign = dict(kept)
        self.dfree = {"sw": [k for i, k in enumerate(self.dkeys) if i < 40 and k not in kept.values()],
                      "hw": [k for i, k in enumerate(self.dkeys) if i >= 40 and k not in kept.values()]}

    def dma(self, q, kind, out, in_, reads=(), writes=()):
        side = writes[0] if kind.startswith(("ld", "wld")) else reads[0]
        cls = "sw" if q == "gpsimd" else "hw"
        if (side.uid, cls) not in self.dassign:
            self.dassign[(side.uid, cls)] = self.dfree[cls].pop(0)
        key = self.dassign[(side.uid, cls)]
        self._pre(q, reads, writes)
        self.cnt[key] += 16
        self.eng(q).dma_start(out=out, in_=in_).then_inc(self.sem[key], 16)
        tok = (key, self.cnt[key])
        self._post(tok, reads, writes)
        return tok


class Ctx:
    pass


_UID = [0]


def _u(name):
    _UID[0] += 1
    return "%s_%d" % (name, _UID[0])


def alloc_linear(C, st):
    nc = C.nc
    C.actT = Buf(st.enter_context(nc.sbuf_tensor(_u("actT"), [128, 16, C.NT], BF16)), "actT")
    C.actT_t = [Buf(C.actT.t, "actT%d" % i) for i in range(C.NTT)]


def _bcast_rows(ap_row, nparts):
    return ap_row.to_broadcast([nparts, ap_row.shape[-1]])


def build_program(layers, NT, seq_start=True):
    assert NT % 1024 == 0
    NTT = NT // 128
    nc = bass.Bass("TRN2", target_bir_lowering=False)
    dr = lambda name, shape, dt, kind="ExternalInput": nc.dram_tensor(name, shape, dt, kind=kind).ap()

    x_in = dr("x", [NT, D], F32)
    out = dr("out", [NT, D], F32, "ExternalOutput")
    norm_w = dr("norm_w", [DEPTH, D], F32)
    out_proj = dr("out_proj", [DEPTH, E, D], F32)
    kinds = sorted(set(l % 3 for l in layers))
    inp = {}
    if 0 in kinds:
        inp["s5_in_proj"] = dr("s5_in_proj", [2, D, 2 * E], F32)
        for nm in ("s5_aT_re", "s5_aT_im", "s5_ldt"):
            inp[nm] = dr(nm, [2, 128, 64], F32)
        for nm in ("s5_bT_re", "s5_bT_im", "s5_cT_re", "s5_cT_im"):
            inp[nm] = dr(nm, [2, 128, 64, 16], F32)
        inp["s5_d"] = dr("s5_d", [2, E], F32)
        inp["s5_w_glu"] = dr("s5_w_glu", [2, E, E], F32)
        inp["s5_b_glu"] = dr("s5_b_glu", [2, E], F32)
    if 1 in kinds:
        inp["fox_in_proj"] = dr("fox_in_proj", [1, D, 4 * E + 16], F32)
        inp["fox_q_norm"] = dr("fox_q_norm", [1, 128], F32)
        inp["fox_k_norm"] = dr("fox_k_norm", [1, 128], F32)
        inp["fox_f_bias"] = dr("fox_f_bias", [1, 16], F32)
    if 2 in kinds:
        inp["pool_in_proj"] = dr("pool_in_proj", [1, D, 2 * E], F32)
        inp["pool_w_group"] = dr("pool_w_group", [1, 4, 512, 512], F32)
        inp["pool_scale"] = dr("pool_scale", [1, E], F32)

    hbufs = [Buf(dr("hbuf%d" % i, [NT, D], F32, "Internal"), "hbuf%d" % i) for i in range(2)]
    proj = Buf(dr("proj", [NT, 4 * E], BF16, "Internal"), "proj")
    ybuf = Buf(dr("ybuf", [NT, E], BF16, "Internal"), "ybuf")
    gbuf = Buf(dr("gbuf", [NT, E], BF16, "Internal"), "gbuf")
    fbuf = Buf(dr("fbuf", [NT, 16], F32, "Internal"), "fbuf")
    xin_b = Buf(x_in, "x")
    out_b = Buf(out, "out")

    with contextlib.ExitStack() as st:
        P = Prog(nc, st)
        C = Ctx()
        C.nc, C.P, C.NT, C.NTT, C.st = nc, P, NT, NTT, st
        sb = lambda name, shape, dt: Buf(st.enter_context(nc.sbuf_tensor(_u(name), shape, dt)), name)
        ps = lambda name, shape, dt: Buf(st.enter_context(nc.psum_tensor(name, shape, dt)), name)
        C.sb, C.ps = sb, ps

        C.ident = sb("ident", [128, 128], BF16)
        C.wblk = [sb("wblk%d" % i, [128, 16, 512], BF16) for i in range(2)]
        C.wb_i = 0
        C.pref = None
        C.bc = sb("bc", [128, 2048], F32)
        C.bc2 = sb("bc2", [128, 2048], F32)
        C.tp = [ps("tp%d" % i, [128, 1024], BF16) for i in range(2)]
        C.acc = [ps("acc%d" % i, [128, 512], F32) for i in range(6)]
        C.acc_i = 0
        C.tp_i = 0
        C.flip = 0

        P.op("gpsimd", lambda e: e.memset(C.ident[:], 1.0), writes=[C.ident])
        P.op("gpsimd", lambda e: e.affine_select(out=C.ident[:], in_=C.ident[:], pattern=[[1, 128]],
                                                  compare_op=ALU.is_equal, fill=0.0, base=0,
                                                  channel_multiplier=-1),
             reads=[C.ident], writes=[C.ident])

        nl = len(layers)
        for li, l in enumerate(layers):
            h_src = xin_b if li == 0 else hbufs[(li - 1) % 2]
            h_dst = out_b if li == nl - 1 else hbufs[li % 2]
            kind, j = l % 3, l // 3
            if kind == 0:
                Wi = inp["s5_in_proj"][j]
                ncols = 2 * E
                zoff = E
            elif kind == 1:
                Wi = inp["fox_in_proj"][0]
                ncols = 4 * E + 16
                zoff = 3 * E
            else:
                Wi = inp["pool_in_proj"][0]
                ncols = 2 * E
                zoff = E
            if li + 1 < nl:
                l2 = layers[li + 1]
                Wnext = (inp["s5_in_proj"][l2 // 3] if l2 % 3 == 0 else
                         inp["fox_in_proj"][0] if l2 % 3 == 1 else inp["pool_in_proj"][0])
            else:
                Wnext = None
            with nc.named_scope("L%d_inproj" % l):
                phase_norm_inproj(C, h_src, norm_w[l:l + 1, :], Wi, ncols, proj, fbuf,
                                  inp["fox_f_bias"][0:1, :] if kind == 1 else None,
                                  qk_rows=(inp["fox_q_norm"][0:1, :], inp["fox_k_norm"][0:1, :]) if kind == 1 else None,
                                  nxt=inp["s5_w_glu"][j] if kind == 0 else out_proj[l])
            if kind == 0:
                with nc.named_scope("L%d_s5mix" % l):
                    s5_mixer(C, j, inp, proj, gbuf, seq_start)
                with nc.named_scope("L%d_s5glu" % l):
                    s5_glu(C, j, inp, gbuf, ybuf, nxt=out_proj[l])
            elif kind == 1:
                with nc.named_scope("L%d_fox" % l):
                    fox_mixer(C, inp, proj, fbuf, ybuf, seq_start)
            else:
                with nc.named_scope("L%d_pool" % l):
                    pool_mixer(C, inp, proj, ybuf, seq_start)
            with nc.named_scope("L%d_outproj" % l):
                phase_gate_outproj(C, ybuf, proj, zoff, out_proj[l], h_src, h_dst, nxt=Wnext)

        for tok in [out_b.w]:
            P.wait("sync", tok)
    return nc


def next_acc(C):
    b = C.acc[C.acc_i % len(C.acc)]
    C.acc_i += 1
    return b


def evac_engine(C):
    C.flip ^= 1
    return "scalar" if C.flip else "vector"


def copy_op(C, e, out_ap, in_ap, reads, writes):
    P = C.P
    if e == "scalar":
        return P.op("scalar", lambda g: g.copy(out=out_ap, in_=in_ap), reads=reads, writes=writes)
    return P.op(e, lambda g: g.tensor_copy(out=out_ap, in_=in_ap), reads=reads, writes=writes)


def transpose_tile(C, src, dstT, col0, nchunks=16, src_off=0, kc_off=0):
    P = C.P
    for g0 in range(0, nchunks, 8):
        n = min(8, nchunks - g0)
        tp = C.tp[C.tp_i % 2]
        C.tp_i += 1
        for q in range(n):
            kc = g0 + q
            P.op("tensor", lambda e, kc=kc, q=q, tp=tp: e.transpose(
                out=tp[:, q * 128:(q + 1) * 128], in_=src[:, src_off + kc * 128: src_off + (kc + 1) * 128],
                identity=C.ident[:]),
                reads=[src, C.ident] if q == 0 else [], writes=[tp] if q == 0 else [], tick=(q == n - 1))
        P.commit("tensor", reads=[src, C.ident], writes=[tp])
        e = evac_engine(C)
        copy_op(C, e, dstT[:, kc_off + g0: kc_off + g0 + n, col0:col0 + 128],
                tp[:, 0:n * 128].rearrange("p (a b) -> p a b", b=128), reads=[tp], writes=[dstT])


def load_wblk(C, W_ap, c0, cw, idx, nk=16, row0=0):
    wb = C.wblk[idx]
    src = W_ap[row0:row0 + nk * 128, c0:c0 + cw].rearrange("(k p) n -> p k n", p=128)
    C.P.dma("gpsimd", "wld", wb[:, 0:nk, 0:cw], src, writes=[wb])
    return wb


def linear(C, W_ap, ncols, epilogue, blocks=None, produce=None, ahead=3, nxt=None):
    P = C.P
    if blocks is None:
        blocks = [(c0, min(512, ncols - c0)) for c0 in range(0, ncols, 512)]
    nb = len(blocks)
    wbs = {}
    if C.pref is not None:
        wbs[0] = C.pref
        idx0 = C.wb_i
        C.pref = None
    else:
        idx0 = C.wb_i
        wbs[0] = load_wblk(C, W_ap, blocks[0][0], blocks[0][1], idx0)
    if produce is not None:
        for i in range(min(ahead, C.NTT)):
            produce(i)
    pending_epi = []
    for bi, (c0, cw) in enumerate(blocks):
        if bi + 1 < nb:
            wbs[bi + 1] = load_wblk(C, W_ap, blocks[bi + 1][0], blocks[bi + 1][1], (idx0 + bi + 1) % 2)
        elif nxt is not None:
            C.wb_i = (idx0 + nb) % 2
            C.pref = load_wblk(C, nxt, 0, 512, C.wb_i)
        wb = wbs[bi]
        for i in range(C.NTT):
            if produce is not None and bi == 0 and i + ahead < C.NTT:
                produce(i + ahead)
            acc = next_acc(C)
            for kc in range(16):
                P.op("tensor", lambda e, kc=kc, i=i, acc=acc, wb=wb, cw=cw: e.matmul(
                    acc[:, 0:cw], lhsT=C.actT[:, kc, i * 128:(i + 1) * 128], rhs=wb[:, kc, 0:cw],
                    start=(kc == 0), stop=(kc == 15)),
                    reads=[C.actT_t[i], wb] if kc == 0 else [], writes=[acc] if kc == 0 else [], tick=False)
            P.commit("tensor", reads=[C.actT_t[i], wb], writes=[acc])
            if pending_epi:
                epilogue(*pending_epi.pop(0))
            pending_epi.append((i, c0, cw, acc))
    while pending_epi:
        epilogue(*pending_epi.pop(0))
    if nxt is None:
        C.wb_i = (idx0 + nb) % 2


def phase_norm_inproj(C, h_src, nw_row, W_ap, ncols, proj, fbuf, fbias_row, qk_rows=None, nxt=None):
    P, nc, NTT = C.P, C.nc, C.NTT
    with contextlib.ExitStack() as st:
        sb = lambda name, shape, dt: Buf(st.enter_context(nc.sbuf_tensor(_u(name), shape, dt)), name)
        alloc_linear(C, st)
        NBF = 3
        ht = [sb("A_h%d" % i, [128, D], F32) for i in range(NBF)]
        junk = sb("A_junk", [128, D], BF16)
        xn = [sb("A_xn%d" % i, [128, D], BF16) for i in range(NBF)]
        stat_all = sb("A_stat", [128, NTT, 4], F32)

        class _View(Buf):
            __slots__ = ("i",)

            def __init__(self, t, i):
                Buf.__init__(self, t, "stat%d" % i)
                self.i = i

            def __getitem__(self, k):
                return self.t[:, self.i, :][k]
        stat = [_View(stat_all.t, i) for i in range(NTT)]
        rstd_all = sb("A_rstd", [128, NTT], F32)
        rstd_t = [Buf(rstd_all.t, "rstd%d" % i) for i in range(NTT)]
        stg = [sb("A_stg%d" % i, [128, 512], BF16) for i in range(4)]
        fst = sb("A_fst", [128, 16], F32)
        fb_bc = sb("A_fb", [128, 16], F32)
        P.dma("sync", "ld", C.bc[:], _bcast_rows(nw_row, 128), writes=[C.bc])
        if fbias_row is not None:
            P.dma("sync", "ld", fb_bc[:], _bcast_rows(fbias_row, 128), writes=[fb_bc])
        if qk_rows is not None:
            qw = sb("A_qw", [128, 128], F32)
            kw = sb("A_kw", [128, 128], F32)
            P.dma("sync", "ld", qw[:], _bcast_rows(qk_rows[0], 128), writes=[qw])
            P.dma("sync", "ld", kw[:], _bcast_rows(qk_rows[1], 128), writes=[kw])
            P.op("vector", lambda e: e.tensor_scalar(out=qw[:], in0=qw[:], scalar1=128.0 ** -0.5, scalar2=None,
                                                      op0=ALU.mult), reads=[qw], writes=[qw])
            P.op("vector", lambda e: e.tensor_tensor(out=qw[:], in0=qw[:], in1=kw[:], op=ALU.mult), reads=[qw, kw], writes=[qw])
            nsq = [sb("A_nsq%d" % i, [128, 512], F32) for i in range(3)]
            nss = [sb("A_nss%d" % i, [128, 4], F32) for i in range(3)]
            ncnt = [0]

        pend = []

        def epi_norm(i, c0, cw, acc, wbc):
            q2, s4 = nsq[ncnt[0] % 3], nss[ncnt[0] % 3]
            ncnt[0] += 1
            hd = lambda ap: ap.rearrange("p (h d) -> p h d", d=128)
            P.op("scalar", lambda e: e.activation(out=q2[:], in_=acc[:, :], func=AF.Square, scale=rstd_all[:, i:i + 1]),
                 reads=[acc, rstd_t[i]], writes=[q2])
            P.op("vector", lambda e: e.tensor_reduce(out=s4[:], in_=hd(q2[:]), axis=AX.X, op=ALU.add),
                 reads=[q2], writes=[s4])
            P.op("vector", lambda e: e.tensor_scalar(out=s4[:], in0=s4[:], scalar1=1.0 / 128, scalar2=EPS,
                                                      op0=ALU.mult, op1=ALU.add), reads=[s4], writes=[s4])

            def part2():
                sg = stg[cnt[0] % 4]
                cnt[0] += 1
                P.op("scalar", lambda e: e.sqrt(out=s4[:], in_=s4[:]), reads=[s4], writes=[s4])
                P.op("vector", lambda e: e.reciprocal(out=s4[:], in_=s4[:]), reads=[s4], writes=[s4])
                P.op("vector", lambda e: e.tensor_scalar(out=s4[:], in0=s4[:], scalar1=rstd_all[:, i:i + 1], scalar2=None,
                                                          op0=ALU.mult), reads=[s4, rstd_t[i]], writes=[s4])
                if wbc is None:
                    P.op("vector", lambda e: e.tensor_tensor(out=hd(sg[:]), in0=hd(acc[:, :]),
                                                              in1=s4[:].unsqueeze(2).to_broadcast([128, 4, 128]), op=ALU.mult),
                         reads=[acc, s4], writes=[sg])
                else:
                    P.op("vector", lambda e: e.tensor_tensor(out=hd(q2[:]), in0=hd(acc[:, :]),
                                                              in1=s4[:].unsqueeze(2).to_broadcast([128, 4, 128]), op=ALU.mult),
                         reads=[acc, s4], writes=[q2])
                    P.op("gpsimd" if i % 2 == 0 else "vector", lambda e: e.tensor_tensor(
                        out=hd(sg[:]), in0=hd(q2[:]), in1=wbc[:].unsqueeze(1).to_broadcast([128, 4, 128]), op=ALU.mult),
                        reads=[q2, wbc], writes=[sg])
                P.dma("sync", "st2", proj[i * 128:(i + 1) * 128, c0:c0 + cw], sg[:, 0:cw], reads=[sg], writes=[proj])

            if pend:
                pend.pop(0)()
            pend.append(part2)

        def tail(i):
            st_ = stat[i]
            P.op("vector", lambda e: e.tensor_scalar(out=st_[:, 1:2], in0=st_[:, 0:1], scalar1=1.0 / D, scalar2=EPS,
                                                      op0=ALU.mult, op1=ALU.add), reads=[st_], writes=[st_])
            P.op("scalar", lambda e: e.sqrt(out=st_[:, 2:3], in_=st_[:, 1:2]), reads=[st_], writes=[st_])
            P.op("vector", lambda e: e.reciprocal(out=rstd_all[:, i:i + 1], in_=st_[:, 2:3]), reads=[st_], writes=[rstd_t[i]])

        def produce(i):
            h = ht[i % NBF]
            st_ = stat[i]
            x = xn[i % NBF]
            P.dma("sync", "ld", h[:], h_src[i * 128:(i + 1) * 128, :], reads=[h_src], writes=[h])
            P.op("scalar", lambda e: e.activation(out=junk[:], in_=h[:], func=AF.Square, accum_out=st_[:, 0:1]),
                 reads=[h], writes=[junk, st_])
            P.op("vector", lambda e: e.tensor_tensor(out=x[:], in0=h[:], in1=C.bc[:], op=ALU.mult),
                 reads=[h, C.bc], writes=[x])
            transpose_tile(C, x, C.actT_t[i], i * 128)

        cnt = [0]

        def epi(i, c0, cw, acc):
            if c0 == 0:
                tail(i)
            if cw == 16:
                P.op("vector", lambda e: e.scalar_tensor_tensor(out=fst[:], in0=acc[:, 0:16], scalar=rstd_all[:, i:i + 1],
                                                                 in1=fb_bc[:], op0=ALU.mult, op1=ALU.add),
                     reads=[acc, fb_bc, rstd_t[i]], writes=[fst])
                P.dma("sync", "st2", fbuf[i * 128:(i + 1) * 128, :], fst[:], reads=[fst], writes=[fbuf])
                return
            if qk_rows is not None and c0 < 2 * E:
                epi_norm(i, c0, cw, acc, qw if c0 < E else None)
                return
            while pend:
                pend.pop(0)()
            sg = stg[cnt[0] % 4]
            cnt[0] += 1
            P.op("scalar", lambda e: e.activation(out=sg[:, 0:cw], in_=acc[:, 0:cw], func=AF.Copy, scale=rstd_all[:, i:i + 1]),
                 reads=[acc, rstd_t[i]], writes=[sg])
            P.dma("scalar", "st2", proj[i * 128:(i + 1) * 128, c0:c0 + cw], sg[:, 0:cw], reads=[sg], writes=[proj])

        linear(C, W_ap, ncols, epi, produce=produce, nxt=nxt)
        while pend:
            pend.pop(0)()
        P.barrier(keep=[C.pref])


def phase_gate_outproj(C, ybuf, proj, zoff, Wo_ap, h_src, h_dst, nxt=None):
    P, nc, NTT = C.P, C.nc, C.NTT
    with contextlib.ExitStack() as st:
        sb = lambda name, shape, dt: Buf(st.enter_context(nc.sbuf_tensor(_u(name), shape, dt)), name)
        alloc_linear(C, st)
        yt = [sb("C_y%d" % i, [128, E], BF16) for i in range(4)]
        zt = [sb("C_z%d" % i, [128, E], BF16) for i in range(4)]
        ym = [sb("C_ym%d" % i, [128, E], BF16) for i in range(4)]
        hs = [sb("C_h%d" % i, [128, 512], F32) for i in range(4)]
        def produce(i):
            y, z, m = yt[i % 4], zt[i % 4], ym[i % 4]
            P.dma("gpsimd", "ld2", y[:], ybuf[i * 128:(i + 1) * 128, :], reads=[ybuf], writes=[y])
            P.dma("gpsimd", "ld2", z[:], proj[i * 128:(i + 1) * 128, zoff:zoff + E], reads=[proj], writes=[z])
            P.op("scalar", lambda e: e.activation(out=z[:], in_=z[:], func=AF.Silu), reads=[z], writes=[z])
            P.op("vector", lambda e: e.tensor_tensor(out=m[:], in0=y[:], in1=z[:], op=ALU.mult),
                 reads=[y, z], writes=[m])
            transpose_tile(C, m, C.actT_t[i], i * 128)

        cnt = [0]

        def epi(i, c0, cw, acc):
            hb = hs[cnt[0] % 4]
            cnt[0] += 1
            P.dma("gpsimd", "ld2", hb[:, 0:cw], h_src[i * 128:(i + 1) * 128, c0:c0 + cw], reads=[h_src], writes=[hb])
            P.op("vector", lambda e: e.tensor_tensor(out=hb[:, 0:cw], in0=acc[:, 0:cw], in1=hb[:, 0:cw], op=ALU.add),
                 reads=[acc, hb], writes=[hb])
            P.dma("sync", "st", h_dst[i * 128:(i + 1) * 128, c0:c0 + cw], hb[:, 0:cw], reads=[hb], writes=[h_dst])

        linear(C, Wo_ap, D, epi, produce=produce, nxt=nxt)
        P.barrier(keep=[C.pref])


def pool_mixer(C, inp, proj, ybuf, seq_start):
    P, nc, NTT = C.P, C.nc, C.NTT
    with contextlib.ExitStack() as st:
        sb = lambda name, shape, dt: Buf(st.enter_context(nc.sbuf_tensor(_u(name), shape, dt)), name)
        alloc_linear(C, st)
        Mt = sb("P_M", [128, 4, 3, 128], BF16)
        tmpf = sb("P_tmpf", [128, 128], F32)
        colsc = sb("P_colsc", [128, 128], F32)
        band = sb("P_band", [128, 128], F32)
        identf = sb("P_identf", [128, 128], F32)
        P.op("vector", lambda e: e.tensor_copy(out=identf[:], in_=C.ident[:]), reads=[C.ident], writes=[identf])
        for g, w in enumerate(POOL_WINDOWS):
            P.op("gpsimd", lambda e: e.memset(band[:], 1.0), writes=[band])
            P.op("gpsimd", lambda e: e.affine_select(out=band[:], in_=band[:], pattern=[[1, 128]], compare_op=ALU.is_ge,
                                                      fill=0.0, base=0, channel_multiplier=-1),
                 reads=[band], writes=[band])
            P.op("gpsimd", lambda e, w=w: e.affine_select(out=band[:], in_=band[:], pattern=[[-1, 128]],
                                                           compare_op=ALU.is_ge, fill=0.0, base=w - 1,
                                                           channel_multiplier=1),
                 reads=[band], writes=[band])
            P.op("vector", lambda e, w=w: e.scalar_tensor_tensor(out=Mt[:, g, 0, :], in0=band[:], scalar=1.0 / w,
                                                                  in1=identf[:], op0=ALU.mult, op1=ALU.subtract),
                 reads=[band, identf], writes=[Mt])
            P.op("gpsimd", lambda e: e.iota(colsc[:], pattern=[[1, 128]], base=1, channel_multiplier=0,
                                            allow_small_or_imprecise_dtypes=True), writes=[colsc])
            P.op("vector", lambda e, w=w: e.tensor_scalar(out=colsc[:], in0=colsc[:], scalar1=float(w), scalar2=None,
                                                           op0=ALU.min), reads=[colsc], writes=[colsc])
            P.op("vector", lambda e: e.reciprocal(out=colsc[:], in_=colsc[:]), reads=[colsc], writes=[colsc])
            P.op("vector", lambda e: e.tensor_tensor(out=tmpf[:], in0=band[:], in1=colsc[:], op=ALU.mult),
                 reads=[band, colsc], writes=[tmpf])
            P.op("vector", lambda e: e.tensor_tensor(out=Mt[:, g, 1, :], in0=tmpf[:], in1=identf[:], op=ALU.subtract),
                 reads=[tmpf, identf], writes=[Mt])
            P.op("gpsimd", lambda e, w=w: e.memset(band[:], 1.0 / w), writes=[band])
            P.op("gpsimd", lambda e, w=w: e.affine_select(out=band[:], in_=band[:], pattern=[[-1, 128]],
                                                           compare_op=ALU.is_gt, fill=0.0, base=w - 128,
                                                           channel_multiplier=1),
                 reads=[band], writes=[band])
            P.op("vector", lambda e: e.tensor_copy(out=Mt[:, g, 2, :], in_=band[:]), reads=[band], writes=[Mt])

        wg = sb("P_wg", [128, 4, 4, 512], BF16)
        for g in range(4):
            P.dma("gpsimd", "wld", wg[:, g, :, :], inp["pool_w_group"][0, g].rearrange("(k p) n -> p k n", p=128),
                  writes=[wg])
        P.dma("sync", "ld", C.bc[:], _bcast_rows(inp["pool_scale"][0:1, :], 128), writes=[C.bc])

        ut = [sb("P_u%d" % i, [128, E], BF16) for i in range(3)]
        for i in range(NTT):
            u = ut[i % 3]
            P.dma("sync", "ld", u[:], proj[i * 128:(i + 1) * 128, 0:E], reads=[proj], writes=[u])
            up = ut[(i - 1) % 3] if i > 0 else None
            for cg in range(4):
                acc = next_acc(C)
                dsel = 1 if (i == 0 and seq_start) else 0
                for cc in range(4):
                    ch = cg * 4 + cc
                    P.op("tensor", lambda e, ch=ch, cc=cc, acc=acc, u=u, cg=cg, dsel=dsel: e.matmul(
                        acc[:, cc * 128:(cc + 1) * 128], lhsT=u[:, ch * 128:(ch + 1) * 128], rhs=Mt[:, cg, dsel, :],
                        start=True, stop=(up is None)),
                        reads=[u, Mt] if cc == 0 else [], writes=[acc] if cc == 0 else [], tick=False)
                    if up is not None:
                        P.op("tensor", lambda e, ch=ch, cc=cc, acc=acc, up=up, cg=cg: e.matmul(
                            acc[:, cc * 128:(cc + 1) * 128], lhsT=up[:, ch * 128:(ch + 1) * 128], rhs=Mt[:, cg, 2, :],
                            start=False, stop=True),
                            reads=[up] if cc == 0 else [], writes=[], tick=False)
                P.commit("tensor", reads=[u, Mt] + ([up] if up is not None else []), writes=[acc])
                copy_op(C, evac_engine(C), C.actT[:, cg * 4:(cg + 1) * 4, i * 128:(i + 1) * 128],
                        acc[:, :].rearrange("p (a b) -> p a b", b=128), reads=[acc], writes=[C.actT_t[i]])

        ystg = [sb("P_y%d" % i, [128, 512], BF16) for i in range(4)]
        k = 0
        for g in range(4):
            for i in range(NTT):
                acc = next_acc(C)
                for cc in range(4):
                    P.op("tensor", lambda e, cc=cc, g=g, i=i, acc=acc: e.matmul(
                        acc[:, :], lhsT=C.actT[:, g * 4 + cc, i * 128:(i + 1) * 128], rhs=wg[:, g, cc, :],
                        start=(cc == 0), stop=(cc == 3)),
                        reads=[C.actT_t[i], wg] if cc == 0 else [], writes=[acc] if cc == 0 else [], tick=False)
                P.commit("tensor", reads=[C.actT_t[i], wg], writes=[acc])
                ys = ystg[k % 4]
                k += 1
                P.op("vector", lambda e, g=g, acc=acc, ys=ys: e.tensor_tensor(
                    out=ys[:], in0=acc[:, :], in1=C.bc[:, g * 512:(g + 1) * 512], op=ALU.mult),
                    reads=[acc, C.bc], writes=[ys])
                P.dma("sync", "st", ybuf[i * 128:(i + 1) * 128, g * 512:(g + 1) * 512], ys[:], reads=[ys], writes=[ybuf])
        P.barrier(keep=[C.pref])


def s5_mixer(C, j, inp, proj, gbuf, seq_start):
    P, nc, NT = C.P, C.nc, C.NT
    NCT = NT // 1024
    NCH = NCT * 128
    NPB = 16
    NB = 64 // NPB
    TWO_PI = 2.0 * math.pi
    with contextlib.ExitStack() as st0:
        sb0 = lambda name, shape, dt: Buf(st0.enter_context(nc.sbuf_tensor(_u(name), shape, dt)), name)
        jv = sb0("S_jv", [128, 24], F32)
        mask01 = sb0("S_mask", [128, 128], F32)
        BS = sb0("S_BS", [128, NPB, 2, 128], BF16)
        K0 = sb0("S_K0", [128, 32, 128], BF16)
        CAre = [sb0("S_CAre%d" % hh, [128, NPB, 8, 16], BF16) for hh in range(2)]
        CAim = [sb0("S_CAim%d" % hh, [128, NPB, 8, 16], BF16) for hh in range(2)]
        lam = sb0("S_lam", [128, 2, NPB], F32)
        io = dict(channel_multiplier=0, allow_small_or_imprecise_dtypes=True)
        P.op("gpsimd", lambda e: e.iota(jv[:, 0:8], pattern=[[-1, 8]], base=7, **io), writes=[jv])
        P.op("gpsimd", lambda e: e.iota(jv[:, 8:16], pattern=[[1, 8]], base=-7, **io), writes=[jv])
        P.op("gpsimd", lambda e: e.iota(jv[:, 16:24], pattern=[[1, 8]], base=1, **io), writes=[jv])
        P.op("gpsimd", lambda e: e.memset(mask01[:], 1.0), writes=[mask01])
        P.op("gpsimd", lambda e: e.affine_select(out=mask01[:].rearrange("p (t c) -> p t c", c=16),
                                                  in_=mask01[:].rearrange("p (t c) -> p t c", c=16),
                                                  pattern=[[16, 8], [0, 16]], compare_op=ALU.is_ge, fill=0.0,
                                                  base=15, channel_multiplier=-1),
             reads=[mask01], writes=[mask01])
        P.dma("sync", "ld", C.bc2[:], _bcast_rows(inp["s5_d"][j:j + 1, :], 128), writes=[C.bc2])

        NPA = 64
        Pr = sb0("S_Pr", [128, NPA, 24], F32)
        Pi = sb0("S_Pi", [128, NPA, 24], F32)
        X16p = lambda name: sb0(name, [128, NPA, 16], F32)
        Br, Bi, c_re, c_im = X16p("S_Br"), X16p("S_Bi"), X16p("S_cre"), X16p("S_cim")
        lam_all = sb0("S_lamall", [128, 2, NPA], F32)
        thc_all = sb0("S_thc", [128, NPA], F32)
        rho_all = sb0("S_rho", [128, NPA], F32)
        mv = sb0("S_mv", [128, 128], F32)
        P.op("gpsimd", lambda e: e.iota(mv[:], pattern=[[1, 128]], base=1, channel_multiplier=0,
                                        allow_small_or_imprecise_dtypes=True), writes=[mv])
        with contextlib.ExitStack() as st1:
            sb1 = lambda name, shape, dt: Buf(st1.enter_context(nc.sbuf_tensor(_u(name), shape, dt)), name)
            V = lambda name: sb1(name, [128, NPA], F32)
            a_re, a_im, ldt, dt_, adr, adi = V("T_are"), V("T_aim"), V("T_ldt"), V("T_dt"), V("T_adr"), V("T_adi")
            xr, den, fr, fi, tv = V("T_xr"), V("T_den"), V("T_fr"), V("T_fi"), V("T_tv")
            W3 = lambda name, dt=F32: sb1(name, [128, NPA, 24], dt)
            argr, mag, ang, tsn, ff, dd, mm = W3("T_argr"), W3("T_mag"), W3("T_ang"), W3("T_tsn"), W3("T_ff"), W3("T_dd"), W3("T_mm")
            ii = W3("T_ii", I32)
            sinv, cosv = W3("T_sin"), W3("T_cos")
            X16 = lambda name: sb1(name, [128, NPA, 16], F32)
            b_re, b_im, x1, x2 = X16("T_bre"), X16("T_bim"), X16("T_x1"), X16("T_x2")
            P.dma("sync", "ld", a_re[:], inp["s5_aT_re"][j, :, :], writes=[a_re])
            P.dma("sync", "ld", a_im[:], inp["s5_aT_im"][j, :, :], writes=[a_im])
            P.dma("sync", "ld", ldt[:], inp["s5_ldt"][j, :, :], writes=[ldt])
            P.dma("sync", "ld", b_re[:], inp["s5_bT_re"][j, :, :, :], writes=[b_re])
            P.dma("sync", "ld", b_im[:], inp["s5_bT_im"][j, :, :, :], writes=[b_im])
            P.dma("sync", "ld", c_re[:], inp["s5_cT_re"][j, :, :, :], writes=[c_re])
            P.dma("sync", "ld", c_im[:], inp["s5_cT_im"][j, :, :, :], writes=[c_im])
            def vtt(out, a, bq, op, reads, writes, eng="vector"):
                P.op(eng, lambda e: e.tensor_tensor(out=out, in0=a, in1=bq, op=op), reads=reads, writes=writes)

            P.op("scalar", lambda e: e.activation(out=dt_[:], in_=ldt[:], func=AF.Exp), reads=[ldt], writes=[dt_])
            vtt(adr[:], a_re[:], dt_[:], ALU.mult, [a_re, dt_], [adr])
            vtt(adi[:], a_im[:], dt_[:], ALU.mult, [a_im, dt_], [adi])
            jvb = jv[:].unsqueeze(1).to_broadcast([128, NPA, 24])
            vtt(argr[:], adr[:].unsqueeze(2).to_broadcast([128, NPA, 24]), jvb, ALU.mult, [adr, jv], [argr])
            P.op("scalar", lambda e: e.activation(out=mag[:], in_=argr[:], func=AF.Exp), reads=[argr], writes=[mag])
            vtt(ang[:], adi[:].unsqueeze(2).to_broadcast([128, NPA, 24]), jvb, ALU.mult, [adi, jv], [ang])
            for off, dst in ((64.0, sinv), (64.25, cosv)):
                P.op("vector", lambda e: e.tensor_scalar(out=tsn[:], in0=ang[:], scalar1=1.0 / TWO_PI, scalar2=off,
                                                          op0=ALU.mult, op1=ALU.add), reads=[ang], writes=[tsn])
                P.op("vector", lambda e: e.tensor_copy(out=ii[:], in_=tsn[:]), reads=[tsn], writes=[ii])
                P.op("vector", lambda e: e.tensor_copy(out=ff[:], in_=ii[:]), reads=[ii], writes=[ff])
                vtt(dd[:], tsn[:], ff[:], ALU.subtract, [tsn, ff], [dd])
                P.op("vector", lambda e: e.tensor_single_scalar(out=mm[:], in_=dd[:], scalar=0.5, op=ALU.is_gt),
                     reads=[dd], writes=[mm])
                vtt(dd[:], dd[:], mm[:], ALU.subtract, [dd, mm], [dd])
                P.op("scalar", lambda e: e.activation(out=dst[:], in_=dd[:], func=AF.Sin, scale=TWO_PI),
                     reads=[dd], writes=[dst])
            vtt(Pr[:], mag[:], cosv[:], ALU.mult, [mag, cosv], [Pr])
            vtt(Pi[:], mag[:], sinv[:], ALU.mult, [mag, sinv], [Pi])
            P.op("vector", lambda e: e.tensor_scalar_add(out=xr[:], in0=Pr[:, :, 16], scalar1=-1.0), reads=[Pr], writes=[xr])
            vtt(den[:], a_re[:], a_re[:], ALU.mult, [a_re], [den])
            vtt(tv[:], a_im[:], a_im[:], ALU.mult, [a_im], [tv])
            vtt(den[:], den[:], tv[:], ALU.add, [den, tv], [den])
            P.op("vector", lambda e: e.reciprocal(out=den[:], in_=den[:]), reads=[den], writes=[den])
            vtt(fr[:], xr[:], a_re[:], ALU.mult, [xr, a_re], [fr])
            vtt(tv[:], Pi[:, :, 16], a_im[:], ALU.mult, [Pi, a_im], [tv])
            vtt(fr[:], fr[:], tv[:], ALU.add, [fr, tv], [fr])
            vtt(fr[:], fr[:], den[:], ALU.mult, [fr, den], [fr])
            vtt(fi[:], Pi[:, :, 16], a_re[:], ALU.mult, [Pi, a_re], [fi])
            vtt(tv[:], xr[:], a_im[:], ALU.mult, [xr, a_im], [tv])
            vtt(fi[:], fi[:], tv[:], ALU.subtract, [fi, tv], [fi])
            vtt(fi[:], fi[:], den[:], ALU.mult, [fi, den], [fi])
            frb = fr[:].unsqueeze(2).to_broadcast([128, NPA, 16])
            fib = fi[:].unsqueeze(2).to_broadcast([128, NPA, 16])
            vtt(x1[:], b_re[:], frb, ALU.mult, [b_re, fr], [x1])
            vtt(x2[:], b_im[:], fib, ALU.mult, [b_im, fi], [x2])
            vtt(Br[:], x1[:], x2[:], ALU.subtract, [x1, x2], [Br])
            vtt(x1[:], b_im[:], frb, ALU.mult, [b_im, fr], [x1])
            vtt(x2[:], b_re[:], fib, ALU.mult, [b_re, fi], [x2])
            vtt(Bi[:], x1[:], x2[:], ALU.add, [x1, x2], [Bi])
            P.op("vector", lambda e: e.tensor_copy(out=lam_all[:, 0, :], in_=Pr[:, :, 23]), reads=[Pr], writes=[lam_all])
            P.op("vector", lambda e: e.tensor_copy(out=lam_all[:, 1, :], in_=Pi[:, :, 23]), reads=[Pi], writes=[lam_all])
            P.op("vector", lambda e: e.tensor_copy(out=rho_all[:], in_=mag[:, :, 23]), reads=[mag], writes=[rho_all])
            P.op("vector", lambda e: e.tensor_scalar(out=tv[:], in0=adi[:], scalar1=8.0 / TWO_PI, scalar2=None, op0=ALU.mult),
                 reads=[adi], writes=[tv])
            P.op("vector", lambda e: e.tensor_copy(out=ii[:, :, 0], in_=tv[:]), reads=[tv], writes=[ii])
            P.op("vector", lambda e: e.tensor_copy(out=xr[:], in_=ii[:, :, 0]), reads=[ii], writes=[xr])
            vtt(thc_all[:], tv[:], xr[:], ALU.subtract, [tv, xr], [thc_all])

            P.barrier(keep=[C.pref])

        def vtt(out, a, bq, op, reads, writes, eng="vector"):
            P.op(eng, lambda e: e.tensor_tensor(out=out, in0=a, in1=bq, op=op), reads=reads, writes=writes)

        for b in range(NB):
            p0 = b * NPB
            ps_ = slice(p0, p0 + NPB)
            with contextlib.ExitStack() as st1:
                sb1 = lambda name, shape, dt: Buf(st1.enter_context(nc.sbuf_tensor(_u(name), shape, dt)), name)
                t1 = [sb1("T_t1%d" % i, [128, NPB, 8, 16], F32) for i in range(2)]
                t2 = [sb1("T_t2%d" % i, [128, NPB, 8, 16], F32) for i in range(2)]
                BSreT = sb1("T_BSreT", [128, NPB, 8, 16], BF16)
                BSimT = sb1("T_BSimT", [128, NPB, 8, 16], BF16)
                CAmre = [sb1("T_CAmre%d" % hh, [128, NPB, 8, 16], BF16) for hh in range(2)]
                CAmim = [sb1("T_CAmim%d" % hh, [128, NPB, 8, 16], BF16) for hh in range(2)]
                P.op("vector", lambda e: e.tensor_copy(out=lam[:], in_=lam_all[:, :, ps_]), reads=[lam_all], writes=[lam])
                tcount = [0]

                def ctab(outb, j0, X, Y, mode):
                    if isinstance(outb, list):
                        ctab(outb[0], j0, X, Y, mode)
                        P.op("scalar", lambda e: e.copy(out=outb[1][64:128], in_=outb[0][64:128]), reads=[outb[0]], writes=[outb[1]])
                        P.op("gpsimd", lambda e: e.memset(outb[1][0:64], 0.0), writes=[outb[1]])
                        P.op("gpsimd", lambda e: e.memset(outb[0][64:128], 0.0), reads=[outb[0]], writes=[outb[0]])
                        return
                    k = tcount[0] % 2
                    tcount[0] += 1
                    A, Bq = (X, Y) if mode == "re" else (Y, X)
                    sh = [128, NPB, 8, 16]
                    prb = Pr[:, ps_, j0:j0 + 8].unsqueeze(3).to_broadcast(sh)
                    pib = Pi[:, ps_, j0:j0 + 8].unsqueeze(3).to_broadcast(sh)
                    vtt(t1[k][:], prb, A[:, ps_].unsqueeze(2).to_broadcast(sh), ALU.mult, [Pr, A], [t1[k]])
                    vtt(t2[k][:], pib, Bq[:, ps_].unsqueeze(2).to_broadcast(sh), ALU.mult, [Pi, Bq], [t2[k]])
                    if mode == "re":
                        vtt(outb[:], t1[k][:], t2[k][:], ALU.subtract, [t1[k], t2[k]], [outb])
                    elif mode == "im":
                        vtt(outb[:], t1[k][:], t2[k][:], ALU.add, [t1[k], t2[k]], [outb])
                    else:
                        P.op("vector", lambda e: e.scalar_tensor_tensor(out=outb[:], in0=t1[k][:], scalar=-1.0, in1=t2[k][:],
                                                                         op0=ALU.mult, op1=ALU.subtract),
                             reads=[t1[k], t2[k]], writes=[outb])

                STOP = 99
                if STOP <= 1:
                    P.barrier(keep=[C.pref])
                    return
                ctab(BSreT, 0, Br, Bi, "re")
                ctab(BSimT, 0, Br, Bi, "im")
                ctab(CAmre, 8, c_re, c_im, "re")
                ctab(CAmim, 8, c_re, c_im, "nim")
                ctab(CAre, 16, c_re, c_im, "re")
                ctab(CAim, 16, c_re, c_im, "nim")

                if STOP <= 2:
                    P.barrier(keep=[C.pref])
                    return
                for jl0 in range(0, NPB, 4):
                    tp = C.tp[C.tp_i % 2]
                    C.tp_i += 1
                    first = True
                    for q in range(4):
                        jl = jl0 + q
                        for ri, tab in ((0, BSreT), (1, BSimT)):
                            P.op("tensor", lambda e: e.transpose(
                                out=tp[:, (q * 2 + ri) * 128:(q * 2 + ri + 1) * 128],
                                in_=tab[:, jl].rearrange("p s c -> p (s c)"), identity=C.ident[:]),
                                reads=[BSreT, BSimT, C.ident] if first else [], writes=[tp] if first else [], tick=False)
                            first = False
                    P.commit("tensor", reads=[BSreT, BSimT, C.ident], writes=[tp])
                    copy_op(C, evac_engine(C), BS[:, jl0:jl0 + 4].rearrange("p g r c -> p (g r c)"), tp[:, :],
                            reads=[tp], writes=[BS])
                for gl0 in range(0, 32, 4):
                    acc = next_acc(C)
                    first = True
                    for q in range(4):
                        gl = gl0 + q
                        jl, hh = gl // 2, gl % 2
                        fl = lambda t: t[:, jl].rearrange("p s c -> p (s c)")
                        P.op("tensor", lambda e: e.matmul(acc[:, q * 128:(q + 1) * 128], lhsT=fl(BSreT), rhs=fl(CAmre[hh]),
                                                          start=True, stop=False),
                             reads=[BSreT, BSimT] + CAmre + CAmim if first else [], writes=[acc] if first else [], tick=False)
                        first = False
                        P.op("tensor", lambda e: e.matmul(acc[:, q * 128:(q + 1) * 128], lhsT=fl(BSimT), rhs=fl(CAmim[hh]),
                                                          start=False, stop=True), tick=False)
                    P.commit("tensor", reads=[BSreT, BSimT] + CAmre + CAmim, writes=[acc])
                    P.op("vector", lambda e: e.tensor_tensor(
                        out=K0[:, gl0:gl0 + 4, :], in0=acc[:, :].rearrange("p (g f) -> p g f", f=128),
                        in1=mask01[:].unsqueeze(1).to_broadcast([128, 4, 128]), op=ALU.mult),
                        reads=[acc, mask01], writes=[K0])
                P.barrier(keep=[C.pref])

            if STOP <= 3:
                return
            with contextlib.ExitStack() as st2:
                sb2 = lambda name, shape, dt: Buf(st2.enter_context(nc.sbuf_tensor(_u(name), shape, dt)), name)
                Uk = sb2("M_Uk", [128, 8, 512], BF16)
                Uk2 = sb2("M_Uk2", [128, 32, 8, 16], BF16)
                UT = sb2("M_UT", [128, 32, 128], BF16)
                Sri = sb2("M_S", [128, 2, NPB, 128], F32)
                Hp = sb2("M_Hp", [128, 2, NPB, 128], BF16)
                T1 = sb2("M_T1", [128, NPB, 128], F32)
                T2 = sb2("M_T2", [128, NPB, 128], F32)
                Rc = sb2("M_Rc", [128, NPB, 128], F32)
                Rs = sb2("M_Rs", [128, NPB, 128], F32)
                iiR = sb2("M_iiR", [128, NPB, 128], I32)
                Hc = sb2("M_Hc", [128, 2, NPB], F32)
                tmp = [sb2("M_tmp%d" % i, [128, 8, 64], F32) for i in range(2)]
                sh3 = [128, NPB, 128]

                def vt(out, a, bq, op, reads, writes):
                    P.op("vector", lambda e: e.tensor_tensor(out=out, in0=a, in1=bq, op=op), reads=reads, writes=writes)

                vt(T1[:], thc_all[:, ps_].unsqueeze(2).to_broadcast(sh3), mv[:].unsqueeze(1).to_broadcast(sh3), ALU.mult,
                   [thc_all, mv], [T1])
                for off, dst in ((0.0, Rs), (0.25, Rc)):
                    if off:
                        P.op("vector", lambda e: e.tensor_scalar_add(out=T1[:], in0=T1[:], scalar1=off), reads=[T1], writes=[T1])
                    P.op("vector", lambda e: e.tensor_copy(out=iiR[:], in_=T1[:]), reads=[T1], writes=[iiR])
                    P.op("vector", lambda e: e.tensor_copy(out=T2[:], in_=iiR[:]), reads=[iiR], writes=[T2])
                    vt(T2[:], T1[:], T2[:], ALU.subtract, [T1, T2], [T2])
                    P.op("scalar", lambda e: e.activation(out=dst[:], in_=T2[:], func=AF.Sin, scale=TWO_PI),
                         reads=[T2], writes=[dst])
                P.op("gpsimd", lambda e: e.memset(Hc[:], 0.0), writes=[Hc])

                ti = 0
                for ct in range(NCT):
                    P.dma("sync", "ld", Uk[:],
                          proj[ct * 1024:(ct + 1) * 1024, b * 512:(b + 1) * 512].rearrange("(k s) c -> k s c", s=8),
                          reads=[proj], writes=[Uk])
                    P.op("scalar", lambda e: e.copy(out=Uk2[:], in_=Uk[:].rearrange("k s (g c) -> k g s c", c=16)),
                         reads=[Uk], writes=[Uk2])
                    for gl0 in range(0, 32, 8):
                        tp = C.tp[C.tp_i % 2]
                        C.tp_i += 1
                        for q in range(8):
                            gl = gl0 + q
                            P.op("tensor", lambda e: e.transpose(out=tp[:, q * 128:(q + 1) * 128],
                                                                 in_=Uk2[:, gl].rearrange("k s c -> k (s c)"),
                                                                 identity=C.ident[:]),
                                 reads=[Uk2, C.ident] if q == 0 else [], writes=[tp] if q == 0 else [], tick=False)
                        P.commit("tensor", reads=[Uk2, C.ident], writes=[tp])
                        copy_op(C, "scalar", UT[:, gl0:gl0 + 8, :],
                                tp[:, :].rearrange("p (a b) -> p a b", b=128), reads=[tp], writes=[UT])
                    for jl0 in range(0, NPB, 2):
                        acc = next_acc(C)
                        first = True
                        for q in range(2):
                            jl = jl0 + q
                            for ri in range(2):
                                slot = q * 2 + ri
                                for hh in range(2):
                                    gl = 2 * jl + hh
                                    P.op("tensor", lambda e: e.matmul(
                                        acc[hh * 64:(hh + 1) * 64, slot * 128:(slot + 1) * 128],
                                        lhsT=BS[:, jl, ri, hh * 64:(hh + 1) * 64], rhs=UT[:, gl, :], start=True, stop=True),
                                        reads=[BS, UT] if first else [], writes=[acc] if first else [], tick=False)
                                    first = False
                        P.commit("tensor", reads=[BS, UT], writes=[acc])
                        copy_op(C, "scalar", Sri[:, :, jl0:jl0 + 2, :].rearrange("p r q k -> p q r k"),
                                acc[:, :].rearrange("p (q r k) -> p q r k", q=2, r=2), reads=[acc], writes=[Sri])
                    Sr, Si = Sri[:, 0], Sri[:, 1]
                    vt(T1[:], Rc[:], Sr, ALU.mult, [Rc, Sri], [T1])
                    vt(T2[:], Rs[:], Si, ALU.mult, [Rs, Sri], [T2])
                    vt(T1[:], T1[:], T2[:], ALU.add, [T1, T2], [T1])
                    vt(T2[:], Rc[:], Si, ALU.mult, [Rc, Sri], [T2])
                    vt(Sr, Rs[:], Sr, ALU.mult, [Rs, Sri], [Sri])
                    vt(T2[:], T2[:], Sr, ALU.subtract, [T2, Sri], [T2])
                    P.op("vector", lambda e: e.tensor_copy(out=Hp[:, :, :, 0:1], in_=Hc[:].unsqueeze(3)),
                         reads=[Hc], writes=[Hp])
                    for bq in (T1, T2, Sri, rho_all, Hc):
                        P._pre("vector", [bq], [])
                    P._pre("vector", [], [Sri])
                    lastins = None
                    for jl in range(NPB):
                        rb = rho_all[:, p0 + jl:p0 + jl + 1].to_broadcast([128, 128])
                        nc.vector.tensor_tensor_scan(out=Sri[:, 0, jl, :], data0=rb, data1=T1[:, jl, :],
                                                     initial=Hc[:, 0, jl:jl + 1], op0=ALU.mult, op1=ALU.add)
                        lastins = nc.vector.tensor_tensor_scan(out=Sri[:, 1, jl, :], data0=rb, data1=T2[:, jl, :],
                                                               initial=Hc[:, 1, jl:jl + 1], op0=ALU.mult, op1=ALU.add)
                    P.last["vector"] = lastins
                    P.commit("vector", reads=[T1, T2, rho_all, Hc], writes=[Sri])
                    vt(T1[:], Rc[:], Sr, ALU.mult, [Rc, Sri], [T1])
                    vt(T2[:], Rs[:], Si, ALU.mult, [Rs, Sri], [T2])
                    vt(Hp[:, 0, :, 1:128], T1[:, :, 0:127], T2[:, :, 0:127], ALU.subtract, [T1, T2], [Hp])
                    vt(Hc[:, 0, :], T1[:, :, 127], T2[:, :, 127], ALU.subtract, [T1, T2], [Hc])
                    vt(T1[:], Rc[:], Si, ALU.mult, [Rc, Sri], [T1])
                    vt(T2[:], Rs[:], Sr, ALU.mult, [Rs, Sri], [T2])
                    vt(Hp[:, 1, :, 1:128], T1[:, :, 0:127], T2[:, :, 0:127], ALU.add, [T1, T2], [Hp])
                    vt(Hc[:, 1, :], T1[:, :, 127], T2[:, :, 127], ALU.add, [T1, T2], [Hc])
                    for gl0 in range(0, 32, 4):
                        acc = next_acc(C)
                        first = True
                        for q in range(4):
                            gl = gl0 + q
                            jl, hh = gl // 2, gl % 2
                            o = acc[:, q * 128:(q + 1) * 128]
                            P.op("tensor", lambda e: e.matmul(o, lhsT=UT[:, gl, :], rhs=K0[:, gl, :], start=True, stop=False),
                                 reads=[UT, K0, Hp] + CAre + CAim if first else [], writes=[acc] if first else [], tick=False)
                            first = False
                            P.op("tensor", lambda e: e.matmul(o, lhsT=Hp[:, 0, jl, :],
                                                              rhs=CAre[hh][:, jl].rearrange("p t c -> p (t c)"),
                                                              start=False, stop=False), tick=False)
                            P.op("tensor", lambda e: e.matmul(o, lhsT=Hp[:, 1, jl, :],
                                                              rhs=CAim[hh][:, jl].rearrange("p t c -> p (t c)"),
                                                              start=False, stop=True), tick=False)
                        P.commit("tensor", reads=[UT, K0, Hp] + CAre + CAim, writes=[acc])
                        ch0 = gl0 * 16
                        t = tmp[ti % 2]
                        ti += 1
                        P.op("vector", lambda e: e.tensor_tensor(
                            out=t[:], in0=Uk[:, :, ch0:ch0 + 64],
                            in1=C.bc2[:, b * 512 + ch0: b * 512 + ch0 + 64].unsqueeze(1).to_broadcast([128, 8, 64]),
                            op=ALU.mult), reads=[Uk, C.bc2], writes=[t])
                        P.op("vector", lambda e: e.tensor_tensor(
                            out=t[:].rearrange("p t (g c) -> p t g c", c=16), in0=t[:].rearrange("p t (g c) -> p t g c", c=16),
                            in1=acc[:, :].rearrange("p (g t c) -> p t g c", g=4, t=8, c=16), op=ALU.add),
                            reads=[t, acc], writes=[t])
                        P.op("scalar", lambda e: e.activation(out=Uk[:, :, ch0:ch0 + 64], in_=t[:],
                                                              func=AF.Gelu_apprx_tanh), reads=[t], writes=[Uk])
                    P.dma("sync", "st",
                          gbuf[ct * 1024:(ct + 1) * 1024, b * 512:(b + 1) * 512].rearrange("(k s) c -> k s c", s=8),
                          Uk[:], reads=[Uk], writes=[gbuf])
                P.barrier(keep=[C.pref])


def s5_glu(C, j, inp, gbuf, ybuf, nxt=None):
    P, nc, NTT = C.P, C.nc, C.NTT
    with contextlib.ExitStack() as st:
        sb = lambda name, shape, dt: Buf(st.enter_context(nc.sbuf_tensor(_u(name), shape, dt)), name)
        alloc_linear(C, st)
        gt = [sb("G_g%d" % i, [128, E], BF16) for i in range(4)]
        gs = [sb("G_gs%d" % i, [128, 512], BF16) for i in range(4)]
        tf = [sb("G_t%d" % i, [128, 512], F32) for i in range(4)]
        ys = [sb("G_y%d" % i, [128, 512], BF16) for i in range(4)]
        P.dma("sync", "ld", C.bc[:], _bcast_rows(inp["s5_b_glu"][j:j + 1, :], 128), writes=[C.bc])
        def produce(i):
            g = gt[i % 4]
            P.dma("gpsimd", "ld2", g[:], gbuf[i * 128:(i + 1) * 128, :], reads=[gbuf], writes=[g])
            transpose_tile(C, g, C.actT_t[i], i * 128)
        cnt = [0]

        def epi(i, c0, cw, acc):
            k = cnt[0] % 4
            cnt[0] += 1
            P.dma("gpsimd", "ld2", gs[k][:, 0:cw], gbuf[i * 128:(i + 1) * 128, c0:c0 + cw], reads=[gbuf], writes=[gs[k]])
            P.op("vector", lambda e: e.tensor_tensor(out=tf[k][:, 0:cw], in0=acc[:, 0:cw], in1=C.bc[:, c0:c0 + cw], op=ALU.add),
                 reads=[acc, C.bc], writes=[tf[k]])
            P.op("scalar", lambda e: e.activation(out=tf[k][:, 0:cw], in_=tf[k][:, 0:cw], func=AF.Sigmoid),
                 reads=[tf[k]], writes=[tf[k]])
            P.op("vector", lambda e: e.tensor_tensor(out=ys[k][:, 0:cw], in0=tf[k][:, 0:cw], in1=gs[k][:, 0:cw], op=ALU.mult),
                 reads=[tf[k], gs[k]], writes=[ys[k]])
            P.dma("sync", "st", ybuf[i * 128:(i + 1) * 128, c0:c0 + cw], ys[k][:, 0:cw], reads=[ys[k]], writes=[ybuf])

        linear(C, inp["s5_w_glu"][j], E, epi, produce=produce, nxt=nxt)
        P.barrier(keep=[C.pref])


def fox_mixer(C, inp, proj, fbuf, ybuf, seq_start):
    P, nc, NT, NTT = C.P, C.nc, C.NT, C.NTT
    scale = 128.0 ** -0.5
    with contextlib.ExitStack() as st:
        sb = lambda name, shape, dt: Buf(st.enter_context(nc.sbuf_tensor(_u(name), shape, dt)), name)
        qw, kw, tri = sb("F_qw", [128, 128], F32), sb("F_kw", [128, 128], F32), sb("F_tri", [128, 128], F32)
        negm = sb("F_negm", [128, 128], BF16)
        AT = sb("F_AT", [16, NT], F32)
        tmpf = sb("F_tmpf", [16, NT], F32)
        carry = sb("F_carry", [16, 2], F32)
        Ahi, Alo, NAhi, NAlo = [sb("F_A%d" % i, [16, NT], BF16) for i in range(4)]
        ft = [sb("F_f%d" % i, [128, 16], F32) for i in range(2)]
        KAz = sb("F_KAz", [128, 4, NT], BF16)
        QAz = sb("F_QAz", [128, 4, NT], BF16)
        qT = sb("F_qT", [128, 4, NT], BF16)
        kT = sb("F_kT", [128, 4, NT], BF16)
        vext = sb("F_v", [128, NTT, 4, 132], BF16)
        xt = [sb("F_x%d" % i, [128, 512], BF16) for i in range(4)]
        sq = [sb("F_sq%d" % i, [128, 512], F32) for i in range(2)]
        xn = [sb("F_xn%d" % i, [128, 512], BF16) for i in range(2)]
        ss = [sb("F_ss%d" % i, [128, 4], F32) for i in range(2)]
        PT = [sb("F_PT%d" % i, [128, 512], BF16) for i in range(3)]
        ystage = [sb("F_ys%d" % i, [128, 4, 128], BF16) for i in range(2)]
        rden = [sb("F_rd%d" % i, [128, 1], F32) for i in range(2)]

        P.dma("sync", "ld", qw[:], _bcast_rows(inp["fox_q_norm"][0:1, :], 128), writes=[qw])
        P.dma("sync", "ld", kw[:], _bcast_rows(inp["fox_k_norm"][0:1, :], 128), writes=[kw])
        P.op("vector", lambda e: e.tensor_scalar(out=qw[:], in0=qw[:], scalar1=scale, scalar2=None, op0=ALU.mult),
             reads=[qw], writes=[qw])
        P.op("gpsimd", lambda e: e.memset(tri[:], 1.0), writes=[tri])
        P.op("gpsimd", lambda e: e.affine_select(out=tri[:], in_=tri[:], pattern=[[1, 128]], compare_op=ALU.is_ge,
                                                  fill=0.0, base=0, channel_multiplier=-1), reads=[tri], writes=[tri])
        P.op("gpsimd", lambda e: e.memset(negm[:], 0.0), writes=[negm])
        P.op("gpsimd", lambda e: e.affine_select(out=negm[:], in_=negm[:], pattern=[[1, 128]], compare_op=ALU.is_ge,
                                                  fill=-30000.0, base=0, channel_multiplier=-1), reads=[negm], writes=[negm])
        P.op("gpsimd", lambda e: e.memset(carry[:], 0.0), writes=[carry])
        P.op("gpsimd", lambda e: e.memset(KAz[:], 0.0), writes=[KAz])
        P.op("gpsimd", lambda e: e.memset(KAz[0:2], 1.0), reads=[KAz], writes=[KAz])
        P.op("gpsimd", lambda e: e.memset(QAz[:], 1.0), writes=[QAz])
        P.op("gpsimd", lambda e: e.memset(vext[:], 1.0), writes=[vext])

        for i in range(NTT):
            f = ft[i % 2]
            P.dma("sync", "ld", f[:], fbuf[i * 128:(i + 1) * 128, :], reads=[fbuf], writes=[f])
            P.op("scalar", lambda e: e.activation(out=f[:], in_=f[:], func=AF.Exp, scale=-1.0), reads=[f], writes=[f])
            P.op("vector", lambda e: e.tensor_scalar_add(out=f[:], in0=f[:], scalar1=1.0), reads=[f], writes=[f])
            P.op("scalar", lambda e: e.activation(out=f[:], in_=f[:], func=AF.Ln), reads=[f], writes=[f])
            acc = C.acc[i % 4]
            P.op("tensor", lambda e: e.matmul(acc[0:16, 0:128], lhsT=f[:, :], rhs=tri[:, :], start=True, stop=True),
                 reads=[f, tri], writes=[acc])
            P.op("vector", lambda e: e.tensor_scalar(out=AT[:, i * 128:(i + 1) * 128], in0=acc[0:16, 0:128],
                                                      scalar1=carry[:, 0:1], scalar2=None, op0=ALU.add),
                 reads=[acc, carry], writes=[AT])
            P.op("vector", lambda e: e.tensor_copy(out=carry[:, 0:1], in_=AT[:, i * 128 + 127:i * 128 + 128]),
                 reads=[AT], writes=[carry])
        P.op("vector", lambda e: e.tensor_copy(out=Ahi[:], in_=AT[:]), reads=[AT], writes=[Ahi])
        P.op("vector", lambda e: e.tensor_copy(out=tmpf[:], in_=Ahi[:]), reads=[Ahi], writes=[tmpf])
        P.op("vector", lambda e: e.tensor_tensor(out=tmpf[:], in0=AT[:], in1=tmpf[:], op=ALU.subtract),
             reads=[AT, tmpf], writes=[tmpf])
        P.op("vector", lambda e: e.tensor_copy(out=Alo[:], in_=tmpf[:]), reads=[tmpf], writes=[Alo])
        P.op("vector", lambda e: e.tensor_scalar(out=NAhi[:], in0=Ahi[:], scalar1=-1.0, scalar2=None, op0=ALU.mult),
             reads=[Ahi], writes=[NAhi])
        P.op("vector", lambda e: e.tensor_scalar(out=NAlo[:], in0=Alo[:], scalar1=-1.0, scalar2=None, op0=ALU.mult),
             reads=[Alo], writes=[NAlo])

        sidx, oidx, pidx, xi, ni = 0, 0, 0, 0, 0
        for hg in range(4):
            for hl in range(4):
                h = hg * 4 + hl
                P.dma("sync", "ld", KAz[2:3, hl, :], Ahi[h:h + 1, :], reads=[Ahi], writes=[KAz])
                P.dma("sync", "ld", KAz[3:4, hl, :], Alo[h:h + 1, :], reads=[Alo], writes=[KAz])
                P.dma("sync", "ld", QAz[0:1, hl, :], NAhi[h:h + 1, :], reads=[NAhi], writes=[QAz])
                P.dma("sync", "ld", QAz[1:2, hl, :], NAlo[h:h + 1, :], reads=[NAlo], writes=[QAz])
            for i in range(NTT):
                rows = slice(i * 128, (i + 1) * 128)
                P.dma("sync", "ld", vext[:, i, :, 0:128],
                      proj[rows, 2 * E + hg * 512: 2 * E + (hg + 1) * 512].rearrange("t (h d) -> t h d", d=128),
                      reads=[proj], writes=[vext])
                for col0, dstT in ((0, qT), (E, kT)):
                    x = xt[xi % 4]
                    xi += 1
                    P.dma("sync", "ld", x[:], proj[rows, col0 + hg * 512: col0 + (hg + 1) * 512], reads=[proj], writes=[x])
                    transpose_tile(C, x, dstT, i * 128, nchunks=4)

            items = []
            for i in range(NTT):
                for hl in range(4):
                    gl = [list(range(j0, min(j0 + 4, i + 1))) for j0 in range(0, i + 1, 4)]
                    for gi, grp in enumerate(gl):
                        items.append((i, hl, grp, gi == len(gl) - 1))

            def emit_S(it):
                nonlocal sidx
                i, hl, grp, _ = it
                qs = slice(i * 128, (i + 1) * 128)
                sacc = C.acc[sidx % 4]
                sidx += 1
                first = True
                for jj, j in enumerate(grp):
                    o = sacc[:, jj * 128:(jj + 1) * 128]
                    ksl = slice(j * 128, (j + 1) * 128)
                    P.op("tensor", lambda e: e.matmul(o, lhsT=kT[:, hl, ksl], rhs=qT[:, hl, qs], start=True, stop=False),
                         reads=[kT, qT, KAz, QAz, negm, C.ident] if first else [], writes=[sacc] if first else [],
                         tick=False)
                    first = False
                    P.op("tensor", lambda e: e.matmul(o, lhsT=KAz[:, hl, ksl], rhs=QAz[:, hl, qs], start=False,
                                                      stop=(j != i)), tick=False)
                    if j == i:
                        P.op("tensor", lambda e: e.matmul(o, lhsT=C.ident[:], rhs=negm[:], start=False, stop=True),
                             tick=False)
                P.commit("tensor", reads=[kT, qT, KAz, QAz, negm, C.ident], writes=[sacc])
                return sacc

            def emit_PV(it, sacc):
                nonlocal pidx
                i, hl, grp, last = it
                qs = slice(i * 128, (i + 1) * 128)
                oacc = C.acc[4 + ((i * 4 + hl) % 2)]
                ys = ystage[i % 2]
                pt = PT[pidx % 3]
                pidx += 1
                n = len(grp)
                P.op("scalar", lambda e: e.activation(out=pt[:, 0:n * 128], in_=sacc[:, 0:n * 128], func=AF.Exp),
                     reads=[sacc], writes=[pt])
                for jj, j in enumerate(grp):
                    P.op("tensor", lambda e: e.matmul(oacc[:, 0:129], lhsT=pt[:, jj * 128:(jj + 1) * 128],
                                                      rhs=vext[:, j, hl, 0:129], start=(j == 0), stop=(j == i)),
                         reads=[pt, vext] if jj == 0 else [], writes=[oacc] if j == 0 else [], tick=False)
                if not last:
                    P.commit("tensor", reads=[pt, vext])
                    return
                P.commit("tensor", reads=[pt, vext], writes=[oacc])
                rd = rden[(i * 4 + hl) % 2]
                P.op("vector", lambda e: e.reciprocal(out=rd[:], in_=oacc[:, 128:129]), reads=[oacc], writes=[rd])
                P.op("vector", lambda e: e.tensor_scalar(out=ys[:, hl, :], in0=oacc[:, 0:128], scalar1=rd[:, 0:1],
                                                          scalar2=None, op0=ALU.mult), reads=[oacc, rd], writes=[ys])
                if hl == 3:
                    P.dma("sync", "st", ybuf[qs, hg * 512:(hg + 1) * 512], ys[:].rearrange("p h d -> p (h d)"),
                          reads=[ys], writes=[ybuf])

            LOOK = 3
            pend = []
            for it in items:
                pend.append((it, emit_S(it)))
                if len(pend) > LOOK:
                    emit_PV(*pend.pop(0))
            while pend:
                emit_PV(*pend.pop(0))
            P.barrier(keep=[C.pref])


def _s5_layouts(inputs):
    o = {}
    f = np.ascontiguousarray
    a_re, a_im, ldt = inputs["s5_a_re"], inputs["s5_a_im"], inputs["s5_log_dt"]
    n = a_re.shape[0]
    o["s5_aT_re"] = f(a_re.reshape(n, 64, 2, 64).transpose(0, 2, 3, 1).reshape(n, 128, 64))
    o["s5_aT_im"] = f(a_im.reshape(n, 64, 2, 64).transpose(0, 2, 3, 1).reshape(n, 128, 64))
    o["s5_ldt"] = f(np.broadcast_to(ldt.reshape(n, 64, 2, 1), (n, 64, 2, 64)).transpose(0, 2, 3, 1).reshape(n, 128, 64))
    for nm in ("b_re", "b_im"):
        b = inputs["s5_" + nm]
        o["s5_bT_" + nm[2:]] = f(b.reshape(n, 64, 2, 64, 16).transpose(0, 2, 3, 1, 4).reshape(n, 128, 64, 16))
    for nm in ("c_re", "c_im"):
        c = inputs["s5_" + nm]
        o["s5_cT_" + nm[2:]] = f(c.reshape(n, 64, 2, 16, 64).transpose(0, 2, 4, 1, 3).reshape(n, 128, 64, 16))
    return o


def make_in_map(inputs, xrows, kinds):
    m = {"x": np.ascontiguousarray(xrows), "norm_w": inputs["norm_w"], "out_proj": inputs["out_proj"]}
    if 0 in kinds:
        m.update(_s5_layouts(inputs))
        for nm in ("s5_in_proj", "s5_d", "s5_w_glu", "s5_b_glu"):
            m[nm] = inputs[nm]
    if 1 in kinds:
        for nm in ("fox_in_proj", "fox_q_norm", "fox_k_norm", "fox_f_bias"):
            m[nm] = inputs[nm]
    if 2 in kinds:
        for nm in ("pool_in_proj", "pool_w_group", "pool_scale"):
            m[nm] = inputs[nm]
    return {k: np.ascontiguousarray(v, dtype=np.float32) for k, v in m.items()}


_PROG_CACHE = {}


def run_layers(inputs, h, layers, ncores=NCORES):
    B, L, _ = h.shape
    key = (tuple(layers), L)
    if key not in _PROG_CACHE:
        _PROG_CACHE[key] = build_program(list(layers), L)
    nc = _PROG_CACHE[key]
    kinds = sorted(set(l % 3 for l in layers))
    in_maps = [make_in_map(inputs, h[c % B], kinds) for c in range(ncores)]
    res = run_bass_kernel_spmd(nc, in_maps, core_ids=list(range(ncores)))
    return np.stack([res.results[c]["out"] for c in range(B)], axis=0)


def kernel(**inputs):
    inputs = {k: np.asarray(v) for k, v in inputs.items()}
    h = inputs["x"].astype(np.float32, copy=False)
    h = run_layers(inputs, h, [0, 1, 2, 3])
    return h.astype(np.float32, copy=False)
```

```python
import contextlib
import math
import os
import numpy as np
import concourse.bass as bass
import concourse.mybir as mybir
from concourse.bass_utils import run_bass_kernel_spmd

F32 = mybir.dt.float32
BF16 = mybir.dt.bfloat16
I32 = mybir.dt.int32
ALU = mybir.AluOpType
AF = mybir.ActivationFunctionType
AX = mybir.AxisListType

D = 2048
E = 2048
DEPTH = 4
EPS = 1e-6
NCORES = 8
POOL_WINDOWS = (2, 4, 8, 16)


class Buf:
    __slots__ = ("t", "w", "r", "name", "uid")
    _n = [0]

    def __init__(self, t, name=""):
        self.t = t
        self.w = None
        self.r = {}
        self.name = name
        Buf._n[0] += 1
        self.uid = Buf._n[0]

    def __getitem__(self, k):
        return self.t[k]


class Prog:
    ENGS = ("sync", "scalar", "vector", "gpsimd", "tensor")

    def __init__(self, nc, st):
        self.nc = nc
        self.st = st
        self.sem = {}
        self.cnt = {}
        for e in self.ENGS:
            self.sem[e] = st.enter_context(nc.semaphore("s_" + e))
            self.cnt[e] = 0
        self.last = {e: None for e in self.ENGS}
        self.seen = {e: {} for e in self.ENGS}
        self.NDS = 96
        self.dkeys = ["dsem%d" % i for i in range(self.NDS)]
        for k in self.dkeys:
            self.sem[k] = st.enter_context(nc.semaphore(k))
            self.cnt[k] = 0
        self.dassign = {}
        self.dfree = {"sw": [k for i, k in enumerate(self.dkeys) if i < 40],
                      "hw": [k for i, k in enumerate(self.dkeys) if i >= 40]}

    def eng(self, e):
        return getattr(self.nc, e)

    def tick(self, e):
        if self.last[e] is None:
            return (e, self.cnt[e])
        self.cnt[e] += 1
        self.last[e].then_inc(self.sem[e], 1)
        self.last[e] = None
        return (e, self.cnt[e])

    def wait(self, e, tok):
        if tok is None:
            return
        key, val = tok
        if val <= 0 or self.seen[e].get(key, 0) >= val:
            return
        if key == e and self.last[e] is not None and self.cnt[e] < val:
            raise RuntimeError("waiting on own future")
        self.eng(e).wait_ge(self.sem[key], val)
        self.seen[e][key] = val

    def _pre(self, e, reads, writes):
        for b in reads:
            self.wait(e, b.w)
        for b in writes:
            self.wait(e, b.w)
            for tok in b.r.values():
                self.wait(e, tok)

    def _post(self, tok, reads, writes):
        for b in writes:
            b.w = tok
            b.r = {}
        for b in reads:
            b.r[tok[0]] = tok

    def op(self, e, fn, reads=(), writes=(), tick=True):
        self._pre(e, reads, writes)
        self.last[e] = fn(self.eng(e))
        if tick:
            tok = self.tick(e)
            self._post(tok, reads, writes)
            return tok
        return None

    def commit(self, e, reads=(), writes=()):
        tok = self.tick(e)
        self._post(tok, reads, writes)
        return tok

    def barrier(self, keep=()):
        kept = {k: v for k, v in self.dassign.items() if any(b is not None and b.uid == k[0] for b in keep)}
        toks = [self.tick(e) for e in self.ENGS]
        toks += [(k, v) for k, v in self.cnt.items() if k not in self.ENGS and k not in kept.values()]
        for e in self.ENGS:
            for t in toks:
                self.wait(e, t)
        self.dassign = dict(kept)
        self.dfree = {"sw": [k for i, k in enumerate(self.dkeys) if i < 40 and k not in kept.values()],
                      "hw": [k for i, k in enumerate(self.dkeys) if i >= 40 and k not in kept.values()]}

    def dma(self, q, kind, out, in_, reads=(), writes=()):
        side = writes[0] if kind.startswith(("ld", "wld")) else reads[0]
        cls = "sw" if q == "gpsimd" else "hw"
        if (side.uid, cls) not in self.dassign:
            self.dassign[(side.uid, cls)] = self.dfree[cls].pop(0)
        key = self.dassign[(side.uid, cls)]
        self._pre(q, reads, writes)
        self.cnt[key] += 16
        self.eng(q).dma_start(out=out, in_=in_).then_inc(self.sem[key], 16)
        tok = (key, self.cnt[key])
        self._post(tok, reads, writes)
        return tok


class Ctx:
    pass


_UID = [0]


def _u(name):
    _UID[0] += 1
    return "%s_%d" % (name, _UID[0])


def alloc_linear(C, st):
    nc = C.nc
    C.actT = Buf(st.enter_context(nc.sbuf_tensor(_u("actT"), [128, 16, C.NT], BF16)), "actT")
    C.actT_t = [Buf(C.actT.t, "actT%d" % i) for i in range(C.NTT)]


def _bcast_rows(ap_row, nparts):
    return ap_row.to_broadcast([nparts, ap_row.shape[-1]])


def build_program(layers, NT, seq_start=True):
    assert NT % 1024 == 0
    NTT = NT // 128
    nc = bass.Bass("TRN2", target_bir_lowering=False)
    dr = lambda name, shape, dt, kind="ExternalInput": nc.dram_tensor(name, shape, dt, kind=kind).ap()

    x_in = dr("x", [NT, D], F32)
    out = dr("out", [NT, D], F32, "ExternalOutput")
    norm_w = dr("norm_w", [DEPTH, D], F32)
    out_proj = dr("out_proj", [DEPTH, E, D], F32)
    kinds = sorted(set(l % 3 for l in layers))
    inp = {}
    if 0 in kinds:
        inp["s5_in_proj"] = dr("s5_in_proj", [2, D, 2 * E], F32)
        for nm in ("s5_aT_re", "s5_aT_im", "s5_ldt"):
            inp[nm] = dr(nm, [2, 128, 64], F32)
        for nm in ("s5_bT_re", "s5_bT_im", "s5_cT_re", "s5_cT_im"):
            inp[nm] = dr(nm, [2, 128, 64, 16], F32)
        inp["s5_d"] = dr("s5_d", [2, E], F32)
        inp["s5_w_glu"] = dr("s5_w_glu", [2, E, E], F32)
        inp["s5_b_glu"] = dr("s5_b_glu", [2, E], F32)
    if 1 in kinds:
        inp["fox_in_proj"] = dr("fox_in_proj", [1, D, 4 * E + 16], F32)
        inp["fox_q_norm"] = dr("fox_q_norm", [1, 128], F32)
        inp["fox_k_norm"] = dr("fox_k_norm", [1, 128], F32)
        inp["fox_f_bias"] = dr("fox_f_bias", [1, 16], F32)
    if 2 in kinds:
        inp["pool_in_proj"] = dr("pool_in_proj", [1, D, 2 * E], F32)
        inp["pool_w_group"] = dr("pool_w_group", [1, 4, 512, 512], F32)
        inp["pool_scale"] = dr("pool_scale", [1, E], F32)

    hbufs = [Buf(dr("hbuf%d" % i, [NT, D], F32, "Internal"), "hbuf%d" % i) for i in range(2)]
    proj = Buf(dr("proj", [NT, 4 * E], BF16, "Internal"), "proj")
    ybuf = Buf(dr("ybuf", [NT, E], BF16, "Internal"), "ybuf")
    gbuf = Buf(dr("gbuf", [NT, E], BF16, "Internal"), "gbuf")
    fbuf = Buf(dr("fbuf", [NT, 16], F32, "Internal"), "fbuf")
    xin_b = Buf(x_in, "x")
    out_b = Buf(out, "out")

    with contextlib.ExitStack() as st:
        P = Prog(nc, st)
        C = Ctx()
        C.nc, C.P, C.NT, C.NTT, C.st = nc, P, NT, NTT, st
        sb = lambda name, shape, dt: Buf(st.enter_context(nc.sbuf_tensor(_u(name), shape, dt)), name)
        ps = lambda name, shape, dt: Buf(st.enter_context(nc.psum_tensor(name, shape, dt)), name)
        C.sb, C.ps = sb, ps

        C.ident = sb("ident", [128, 128], BF16)
        C.wblk = [sb("wblk%d" % i, [128, 16, 512], BF16) for i in range(2)]
        C.wb_i = 0
        C.pref = None
        C.bc = sb("bc", [128, 2048], F32)
        C.bc2 = sb("bc2", [128, 2048], F32)
        C.tp = [ps("tp%d" % i, [128, 1024], BF16) for i in range(2)]
        C.acc = [ps("acc%d" % i, [128, 512], F32) for i in range(6)]
        C.acc_i = 0
        C.tp_i = 0
        C.flip = 0

        P.op("gpsimd", lambda e: e.memset(C.ident[:], 1.0), writes=[C.ident])
        P.op("gpsimd", lambda e: e.affine_select(out=C.ident[:], in_=C.ident[:], pattern=[[1, 128]],
                                                  compare_op=ALU.is_equal, fill=0.0, base=0,
                                                  channel_multiplier=-1),
             reads=[C.ident], writes=[C.ident])

        nl = len(layers)
        for li, l in enumerate(layers):
            h_src = xin_b if li == 0 else hbufs[(li - 1) % 2]
            h_dst = out_b if li == nl - 1 else hbufs[li % 2]
            kind, j = l % 3, l // 3
            if kind == 0:
                Wi = inp["s5_in_proj"][j]
                ncols = 2 * E
                zoff = E
            elif kind == 1:
                Wi = inp["fox_in_proj"][0]
                ncols = 4 * E + 16
                zoff = 3 * E
            else:
                Wi = inp["pool_in_proj"][0]
                ncols = 2 * E
                zoff = E
            if li + 1 < nl:
                l2 = layers[li + 1]
                Wnext = (inp["s5_in_proj"][l2 // 3] if l2 % 3 == 0 else
                         inp["fox_in_proj"][0] if l2 % 3 == 1 else inp["pool_in_proj"][0])
            else:
                Wnext = None
            with nc.named_scope("L%d_inproj" % l):
                phase_norm_inproj(C, h_src, norm_w[l:l + 1, :], Wi, ncols, proj, fbuf,
                                  inp["fox_f_bias"][0:1, :] if kind == 1 else None,
                                  qk_rows=(inp["fox_q_norm"][0:1, :], inp["fox_k_norm"][0:1, :]) if kind == 1 else None,
                                  nxt=inp["s5_w_glu"][j] if kind == 0 else out_proj[l])
            if kind == 0:
                with nc.named_scope("L%d_s5mix" % l):
                    s5_mixer(C, j, inp, proj, gbuf, seq_start)
                with nc.named_scope("L%d_s5glu" % l):
                    s5_glu(C, j, inp, gbuf, ybuf, nxt=out_proj[l])
            elif kind == 1:
                with nc.named_scope("L%d_fox" % l):
                    fox_mixer(C, inp, proj, fbuf, ybuf, seq_start)
            else:
                with nc.named_scope("L%d_pool" % l):
                    pool_mixer(C, inp, proj, ybuf, seq_start)
            with nc.named_scope("L%d_outproj" % l):
                phase_gate_outproj(C, ybuf, proj, zoff, out_proj[l], h_src, h_dst, nxt=Wnext)

        for tok in [out_b.w]:
            P.wait("sync", tok)
    return nc


def next_acc(C):
    b = C.acc[C.acc_i % len(C.acc)]
    C.acc_i += 1
    return b


def evac_engine(C):
    C.flip ^= 1
    return "scalar" if C.flip else "vector"


def copy_op(C, e, out_ap, in_ap, reads, writes):
    P = C.P
    if e == "scalar":
        return P.op("scalar", lambda g: g.copy(out=out_ap, in_=in_ap), reads=reads, writes=writes)
    return P.op(e, lambda g: g.tensor_copy(out=out_ap, in_=in_ap), reads=reads, writes=writes)


def transpose_tile(C, src, dstT, col0, nchunks=16, src_off=0, kc_off=0):
    P = C.P
    for g0 in range(0, nchunks, 8):
        n = min(8, nchunks - g0)
        tp = C.tp[C.tp_i % 2]
        C.tp_i += 1
        for q in range(n):
            kc = g0 + q
            P.op("tensor", lambda e, kc=kc, q=q, tp=tp: e.transpose(
                out=tp[:, q * 128:(q + 1) * 128], in_=src[:, src_off + kc * 128: src_off + (kc + 1) * 128],
                identity=C.ident[:]),
                reads=[src, C.ident] if q == 0 else [], writes=[tp] if q == 0 else [], tick=(q == n - 1))
        P.commit("tensor", reads=[src, C.ident], writes=[tp])
        e = evac_engine(C)
        copy_op(C, e, dstT[:, kc_off + g0: kc_off + g0 + n, col0:col0 + 128],
                tp[:, 0:n * 128].rearrange("p (a b) -> p a b", b=128), reads=[tp], writes=[dstT])


def load_wblk(C, W_ap, c0, cw, idx, nk=16, row0=0):
    wb = C.wblk[idx]
    src = W_ap[row0:row0 + nk * 128, c0:c0 + cw].rearrange("(k p) n -> p k n", p=128)
    C.P.dma("gpsimd", "wld", wb[:, 0:nk, 0:cw], src, writes=[wb])
    return wb


def linear(C, W_ap, ncols, epilogue, blocks=None, produce=None, ahead=3, nxt=None):
    P = C.P
    if blocks is None:
        blocks = [(c0, min(512, ncols - c0)) for c0 in range(0, ncols, 512)]
    nb = len(blocks)
    wbs = {}
    if C.pref is not None:
        wbs[0] = C.pref
        idx0 = C.wb_i
        C.pref = None
    else:
        idx0 = C.wb_i
        wbs[0] = load_wblk(C, W_ap, blocks[0][0], blocks[0][1], idx0)
    if produce is not None:
        for i in range(min(ahead, C.NTT)):
            produce(i)
    pending_epi = []
    for bi, (c0, cw) in enumerate(blocks):
        if bi + 1 < nb:
            wbs[bi + 1] = load_wblk(C, W_ap, blocks[bi + 1][0], blocks[bi + 1][1], (idx0 + bi + 1) % 2)
        elif nxt is not None:
            C.wb_i = (idx0 + nb) % 2
            C.pref = load_wblk(C, nxt, 0, 512, C.wb_i)
        wb = wbs[bi]
        for i in range(C.NTT):
            if produce is not None and bi == 0 and i + ahead < C.NTT:
                produce(i + ahead)
            acc = next_acc(C)
            for kc in range(16):
                P.op("tensor", lambda e, kc=kc, i=i, acc=acc, wb=wb, cw=cw: e.matmul(
                    acc[:, 0:cw], lhsT=C.actT[:, kc, i * 128:(i + 1) * 128], rhs=wb[:, kc, 0:cw],
                    start=(kc == 0), stop=(kc == 15)),
                    reads=[C.actT_t[i], wb] if kc == 0 else [], writes=[acc] if kc == 0 else [], tick=False)
            P.commit("tensor", reads=[C.actT_t[i], wb], writes=[acc])
            if pending_epi:
                epilogue(*pending_epi.pop(0))
            pending_epi.append((i, c0, cw, acc))
    while pending_epi:
        epilogue(*pending_epi.pop(0))
    if nxt is None:
        C.wb_i = (idx0 + nb) % 2


def phase_norm_inproj(C, h_src, nw_row, W_ap, ncols, proj, fbuf, fbias_row, qk_rows=None, nxt=None):
    P, nc, NTT = C.P, C.nc, C.NTT
    with contextlib.ExitStack() as st:
        sb = lambda name, shape, dt: Buf(st.enter_context(nc.sbuf_tensor(_u(name), shape, dt)), name)
        alloc_linear(C, st)
        NBF = 3
        ht = [sb("A_h%d" % i, [128, D], F32) for i in range(NBF)]
        junk = sb("A_junk", [128, D], BF16)
        xn = [sb("A_xn%d" % i, [128, D], BF16) for i in range(NBF)]
        stat_all = sb("A_stat", [128, NTT, 4], F32)

        class _View(Buf):
            __slots__ = ("i",)

            def __init__(self, t, i):
                Buf.__init__(self, t, "stat%d" % i)
                self.i = i

            def __getitem__(self, k):
                return self.t[:, self.i, :][k]
        stat = [_View(stat_all.t, i) for i in range(NTT)]
        rstd_all = sb("A_rstd", [128, NTT], F32)
        rstd_t = [Buf(rstd_all.t, "rstd%d" % i) for i in range(NTT)]
        stg = [sb("A_stg%d" % i, [128, 512], BF16) for i in range(4)]
        fst = sb("A_fst", [128, 16], F32)
        fb_bc = sb("A_fb", [128, 16], F32)
        P.dma("sync", "ld", C.bc[:], _bcast_rows(nw_row, 128), writes=[C.bc])
        if fbias_row is not None:
            P.dma("sync", "ld", fb_bc[:], _bcast_rows(fbias_row, 128), writes=[fb_bc])
        if qk_rows is not None:
            qw = sb("A_qw", [128, 128], F32)
            kw = sb("A_kw", [128, 128], F32)
            P.dma("sync", "ld", qw[:], _bcast_rows(qk_rows[0], 128), writes=[qw])
            P.dma("sync", "ld", kw[:], _bcast_rows(qk_rows[1], 128), writes=[kw])
            P.op("vector", lambda e: e.tensor_scalar(out=qw[:], in0=qw[:], scalar1=128.0 ** -0.5, scalar2=None,
                                                      op0=ALU.mult), reads=[qw], writes=[qw])
            P.op("vector", lambda e: e.tensor_tensor(out=qw[:], in0=qw[:], in1=kw[:], op=ALU.mult), reads=[qw, kw], writes=[qw])
            nsq = [sb("A_nsq%d" % i, [128, 512], F32) for i in range(3)]
            nss = [sb("A_nss%d" % i, [128, 4], F32) for i in range(3)]
            ncnt = [0]

        pend = []

        def epi_norm(i, c0, cw, acc, wbc):
            q2, s4 = nsq[ncnt[0] % 3], nss[ncnt[0] % 3]
            ncnt[0] += 1
            hd = lambda ap: ap.rearrange("p (h d) -> p h d", d=128)
            P.op("scalar", lambda e: e.activation(out=q2[:], in_=acc[:, :], func=AF.Square, scale=rstd_all[:, i:i + 1]),
                 reads=[acc, rstd_t[i]], writes=[q2])
            P.op("vector", lambda e: e.tensor_reduce(out=s4[:], in_=hd(q2[:]), axis=AX.X, op=ALU.add),
                 reads=[q2], writes=[s4])
            P.op("vector", lambda e: e.tensor_scalar(out=s4[:], in0=s4[:], scalar1=1.0 / 128, scalar2=EPS,
                                                      op0=ALU.mult, op1=ALU.add), reads=[s4], writes=[s4])

            def part2():
                sg = stg[cnt[0] % 4]
                cnt[0] += 1
                P.op("scalar", lambda e: e.sqrt(out=s4[:], in_=s4[:]), reads=[s4], writes=[s4])
                P.op("vector", lambda e: e.reciprocal(out=s4[:], in_=s4[:]), reads=[s4], writes=[s4])
                P.op("vector", lambda e: e.tensor_scalar(out=s4[:], in0=s4[:], scalar1=rstd_all[:, i:i + 1], scalar2=None,
                                                          op0=ALU.mult), reads=[s4, rstd_t[i]], writes=[s4])
                if wbc is None:
                    P.op("vector", lambda e: e.tensor_tensor(out=hd(sg[:]), in0=hd(acc[:, :]),
                                                              in1=s4[:].unsqueeze(2).to_broadcast([128, 4, 128]), op=ALU.mult),
                         reads=[acc, s4], writes=[sg])
                else:
                    P.op("vector", lambda e: e.tensor_tensor(out=hd(q2[:]), in0=hd(acc[:, :]),
                                                              in1=s4[:].unsqueeze(2).to_broadcast([128, 4, 128]), op=ALU.mult),
                         reads=[acc, s4], writes=[q2])
                    P.op("gpsimd" if i % 2 == 0 else "vector", lambda e: e.tensor_tensor(
                        out=hd(sg[:]), in0=hd(q2[:]), in1=wbc[:].unsqueeze(1).to_broadcast([128, 4, 128]), op=ALU.mult),
                        reads=[q2, wbc], writes=[sg])
                P.dma("sync", "st2", proj[i * 128:(i + 1) * 128, c0:c0 + cw], sg[:, 0:cw], reads=[sg], writes=[proj])

            if pend:
                pend.pop(0)()
            pend.append(part2)

        def tail(i):
            st_ = stat[i]
            P.op("vector", lambda e: e.tensor_scalar(out=st_[:, 1:2], in0=st_[:, 0:1], scalar1=1.0 / D, scalar2=EPS,
                                                      op0=ALU.mult, op1=ALU.add), reads=[st_], writes=[st_])
            P.op("scalar", lambda e: e.sqrt(out=st_[:, 2:3], in_=st_[:, 1:2]), reads=[st_], writes=[st_])
            P.op("vector", lambda e: e.reciprocal(out=rstd_all[:, i:i + 1], in_=st_[:, 2:3]), reads=[st_], writes=[rstd_t[i]])

        def produce(i):
            h = ht[i % NBF]
            st_ = stat[i]
            x = xn[i % NBF]
            P.dma("sync", "ld", h[:], h_src[i * 128:(i + 1) * 128, :], reads=[h_src], writes=[h])
            P.op("scalar", lambda e: e.activation(out=junk[:], in_=h[:], func=AF.Square, accum_out=st_[:, 0:1]),
                 reads=[h], writes=[junk, st_])
            P.op("vector", lambda e: e.tensor_tensor(out=x[:], in0=h[:], in1=C.bc[:], op=ALU.mult),
                 reads=[h, C.bc], writes=[x])
            transpose_tile(C, x, C.actT_t[i], i * 128)

        cnt = [0]

        def epi(i, c0, cw, acc):
            if c0 == 0:
                tail(i)
            if cw == 16:
                P.op("vector", lambda e: e.scalar_tensor_tensor(out=fst[:], in0=acc[:, 0:16], scalar=rstd_all[:, i:i + 1],
                                                                 in1=fb_bc[:], op0=ALU.mult, op1=ALU.add),
                     reads=[acc, fb_bc, rstd_t[i]], writes=[fst])
                P.dma("sync", "st2", fbuf[i * 128:(i + 1) * 128, :], fst[:], reads=[fst], writes=[fbuf])
                return
            if qk_rows is not None and c0 < 2 * E:
                epi_norm(i, c0, cw, acc, qw if c0 < E else None)
                return
            while pend:
                pend.pop(0)()
            sg = stg[cnt[0] % 4]
            cnt[0] += 1
            P.op("scalar", lambda e: e.activation(out=sg[:, 0:cw], in_=acc[:, 0:cw], func=AF.Copy, scale=rstd_all[:, i:i + 1]),
                 reads=[acc, rstd_t[i]], writes=[sg])
            P.dma("scalar", "st2", proj[i * 128:(i + 1) * 128, c0:c0 + cw], sg[:, 0:cw], reads=[sg], writes=[proj])

        linear(C, W_ap, ncols, epi, produce=produce, nxt=nxt)
        while pend:
            pend.pop(0)()
        P.barrier(keep=[C.pref])


def phase_gate_outproj(C, ybuf, proj, zoff, Wo_ap, h_src, h_dst, nxt=None):
    P, nc, NTT = C.P, C.nc, C.NTT
    with contextlib.ExitStack() as st:
        sb = lambda name, shape, dt: Buf(st.enter_context(nc.sbuf_tensor(_u(name), shape, dt)), name)
        alloc_linear(C, st)
        yt = [sb("C_y%d" % i, [128, E], BF16) for i in range(4)]
        zt = [sb("C_z%d" % i, [128, E], BF16) for i in range(4)]
        ym = [sb("C_ym%d" % i, [128, E], BF16) for i in range(4)]
        hs = [sb("C_h%d" % i, [128, 512], F32) for i in range(4)]
        def produce(i):
            y, z, m = yt[i % 4], zt[i % 4], ym[i % 4]
            P.dma("gpsimd", "ld2", y[:], ybuf[i * 128:(i + 1) * 128, :], reads=[ybuf], writes=[y])
            P.dma("gpsimd", "ld2", z[:], proj[i * 128:(i + 1) * 128, zoff:zoff + E], reads=[proj], writes=[z])
            P.op("scalar", lambda e: e.activation(out=z[:], in_=z[:], func=AF.Silu), reads=[z], writes=[z])
            P.op("vector", lambda e: e.tensor_tensor(out=m[:], in0=y[:], in1=z[:], op=ALU.mult),
                 reads=[y, z], writes=[m])
            transpose_tile(C, m, C.actT_t[i], i * 128)

        cnt = [0]

        def epi(i, c0, cw, acc):
            hb = hs[cnt[0] % 4]
            cnt[0] += 1
            P.dma("gpsimd", "ld2", hb[:, 0:cw], h_src[i * 128:(i + 1) * 128, c0:c0 + cw], reads=[h_src], writes=[hb])
            P.op("vector", lambda e: e.tensor_tensor(out=hb[:, 0:cw], in0=acc[:, 0:cw], in1=hb[:, 0:cw], op=ALU.add),
                 reads=[acc, hb], writes=[hb])
            P.dma("sync", "st", h_dst[i * 128:(i + 1) * 128, c0:c0 + cw], hb[:, 0:cw], reads=[hb], writes=[h_dst])

        linear(C, Wo_ap, D, epi, produce=produce, nxt=nxt)
        P.barrier(keep=[C.pref])


def pool_mixer(C, inp, proj, ybuf, seq_start):
    P, nc, NTT = C.P, C.nc, C.NTT
    with contextlib.ExitStack() as st:
        sb = lambda name, shape, dt: Buf(st.enter_context(nc.sbuf_tensor(_u(name), shape, dt)), name)
        alloc_linear(C, st)
        Mt = sb("P_M", [128, 4, 3, 128], BF16)
        tmpf = sb("P_tmpf", [128, 128], F32)
        colsc = sb("P_colsc", [128, 128], F32)
        band = sb("P_band", [128, 128], F32)
        identf = sb("P_identf", [128, 128], F32)
        P.op("vector", lambda e: e.tensor_copy(out=identf[:], in_=C.ident[:]), reads=[C.ident], writes=[identf])
        for g, w in enumerate(POOL_WINDOWS):
            P.op("gpsimd", lambda e: e.memset(band[:], 1.0), writes=[band])
            P.op("gpsimd", lambda e: e.affine_select(out=band[:], in_=band[:], pattern=[[1, 128]], compare_op=ALU.is_ge,
                                                      fill=0.0, base=0, channel_multiplier=-1),
                 reads=[band], writes=[band])
            P.op("gpsimd", lambda e, w=w: e.affine_select(out=band[:], in_=band[:], pattern=[[-1, 128]],
                                                           compare_op=ALU.is_ge, fill=0.0, base=w - 1,
                                                           channel_multiplier=1),
                 reads=[band], writes=[band])
            P.op("vector", lambda e, w=w: e.scalar_tensor_tensor(out=Mt[:, g, 0, :], in0=band[:], scalar=1.0 / w,
                                                                  in1=identf[:], op0=ALU.mult, op1=ALU.subtract),
                 reads=[band, identf], writes=[Mt])
            P.op("gpsimd", lambda e: e.iota(colsc[:], pattern=[[1, 128]], base=1, channel_multiplier=0,
                                            allow_small_or_imprecise_dtypes=True), writes=[colsc])
            P.op("vector", lambda e, w=w: e.tensor_scalar(out=colsc[:], in0=colsc[:], scalar1=float(w), scalar2=None,
                                                           op0=ALU.min), reads=[colsc], writes=[colsc])
            P.op("vector", lambda e: e.reciprocal(out=colsc[:], in_=colsc[:]), reads=[colsc], writes=[colsc])
            P.op("vector", lambda e: e.tensor_tensor(out=tmpf[:], in0=band[:], in1=colsc[:], op=ALU.mult),
                 reads=[band, colsc], writes=[tmpf])
            P.op("vector", lambda e: e.tensor_tensor(out=Mt[:, g, 1, :], in0=tmpf[:], in1=identf[:], op=ALU.subtract),
                 reads=[tmpf, identf], writes=[Mt])
            P.op("gpsimd", lambda e, w=w: e.memset(band[:], 1.0 / w), writes=[band])
            P.op("gpsimd", lambda e, w=w: e.affine_select(out=band[:], in_=band[:], pattern=[[-1, 128]],
                                                           compare_op=ALU.is_gt, fill=0.0, base=w - 128,
                                                           channel_multiplier=1),
                 reads=[band], writes=[band])
            P.op("vector", lambda e: e.tensor_copy(out=Mt[:, g, 2, :], in_=band[:]), reads=[band], writes=[Mt])

        wg = sb("P_wg", [128, 4, 4, 512], BF16)
        for g in range(4):
            P.dma("gpsimd", "wld", wg[:, g, :, :], inp["pool_w_group"][0, g].rearrange("(k p) n -> p k n", p=128),
                  writes=[wg])
        P.dma("sync", "ld", C.bc[:], _bcast_rows(inp["pool_scale"][0:1, :], 128), writes=[C.bc])

        ut = [sb("P_u%d" % i, [128, E], BF16) for i in range(3)]
        for i in range(NTT):
            u = ut[i % 3]
            P.dma("sync", "ld", u[:], proj[i * 128:(i + 1) * 128, 0:E], reads=[proj], writes=[u])
            up = ut[(i - 1) % 3] if i > 0 else None
            for cg in range(4):
                acc = next_acc(C)
                dsel = 1 if (i == 0 and seq_start) else 0
                for cc in range(4):
                    ch = cg * 4 + cc
                    P.op("tensor", lambda e, ch=ch, cc=cc, acc=acc, u=u, cg=cg, dsel=dsel: e.matmul(
                        acc[:, cc * 128:(cc + 1) * 128], lhsT=u[:, ch * 128:(ch + 1) * 128], rhs=Mt[:, cg, dsel, :],
                        start=True, stop=(up is None)),
                        reads=[u, Mt] if cc == 0 else [], writes=[acc] if cc == 0 else [], tick=False)
                    if up is not None:
                        P.op("tensor", lambda e, ch=ch, cc=cc, acc=acc, up=up, cg=cg: e.matmul(
                            acc[:, cc * 128:(cc + 1) * 128], lhsT=up[:, ch * 128:(ch + 1) * 128], rhs=Mt[:, cg, 2, :],
                            start=False, stop=True),
                            reads=[up] if cc == 0 else [], writes=[], tick=False)
                P.commit("tensor", reads=[u, Mt] + ([up] if up is not None else []), writes=[acc])
                copy_op(C, evac_engine(C), C.actT[:, cg * 4:(cg + 1) * 4, i * 128:(i + 1) * 128],
                        acc[:, :].rearrange("p (a b) -> p a b", b=128), reads=[acc], writes=[C.actT_t[i]])

        ystg = [sb("P_y%d" % i, [128, 512], BF16) for i in range(4)]
        k = 0
        for g in range(4):
            for i in range(NTT):
                acc = next_acc(C)
                for cc in range(4):
                    P.op("tensor", lambda e, cc=cc, g=g, i=i, acc=acc: e.matmul(
                        acc[:, :], lhsT=C.actT[:, g * 4 + cc, i * 128:(i + 1) * 128], rhs=wg[:, g, cc, :],
                        start=(cc == 0), stop=(cc == 3)),
                        reads=[C.actT_t[i], wg] if cc == 0 else [], writes=[acc] if cc == 0 else [], tick=False)
                P.commit("tensor", reads=[C.actT_t[i], wg], writes=[acc])
                ys = ystg[k % 4]
                k += 1
                P.op("vector", lambda e, g=g, acc=acc, ys=ys: e.tensor_tensor(
                    out=ys[:], in0=acc[:, :], in1=C.bc[:, g * 512:(g + 1) * 512], op=ALU.mult),
                    reads=[acc, C.bc], writes=[ys])
                P.dma("sync", "st", ybuf[i * 128:(i + 1) * 128, g * 512:(g + 1) * 512], ys[:], reads=[ys], writes=[ybuf])
        P.barrier(keep=[C.pref])


def s5_mixer(C, j, inp, proj, gbuf, seq_start):
    P, nc, NT = C.P, C.nc, C.NT
    NCT = NT // 1024
    NCH = NCT * 128
    NPB = 16
    NB = 64 // NPB
    TWO_PI = 2.0 * math.pi
    SIN_SCALE = TWO_PI * (1.0 - 1e-6)
    with contextlib.ExitStack() as st0:
        sb0 = lambda name, shape, dt: Buf(st0.enter_context(nc.sbuf_tensor(_u(name), shape, dt)), name)
        jv = sb0("S_jv", [128, 24], F32)
        mask01 = sb0("S_mask", [128, 128], F32)
        BS = sb0("S_BS", [128, NPB, 2, 128], BF16)
        K0 = sb0("S_K0", [128, 32, 128], BF16)
        CAre = [sb0("S_CAre%d" % hh, [128, NPB, 8, 16], BF16) for hh in range(2)]
        CAim = [sb0("S_CAim%d" % hh, [128, NPB, 8, 16], BF16) for hh in range(2)]
        lam = sb0("S_lam", [128, 2, NPB], F32)
        io = dict(channel_multiplier=0, allow_small_or_imprecise_dtypes=True)
        P.op("gpsimd", lambda e: e.iota(jv[:, 0:8], pattern=[[-1, 8]], base=7, **io), writes=[jv])
        P.op("gpsimd", lambda e: e.iota(jv[:, 8:16], pattern=[[1, 8]], base=-7, **io), writes=[jv])
        P.op("gpsimd", lambda e: e.iota(jv[:, 16:24], pattern=[[1, 8]], base=1, **io), writes=[jv])
        P.op("gpsimd", lambda e: e.memset(mask01[:], 1.0), writes=[mask01])
        P.op("gpsimd", lambda e: e.affine_select(out=mask01[:].rearrange("p (t c) -> p t c", c=16),
                                                  in_=mask01[:].rearrange("p (t c) -> p t c", c=16),
                                                  pattern=[[16, 8], [0, 16]], compare_op=ALU.is_ge, fill=0.0,
                                                  base=15, channel_multiplier=-1),
             reads=[mask01], writes=[mask01])
        P.dma("sync", "ld", C.bc2[:], _bcast_rows(inp["s5_d"][j:j + 1, :], 128), writes=[C.bc2])

        NPA = 64
        Pr = sb0("S_Pr", [128, NPA, 24], F32)
        Pi = sb0("S_Pi", [128, NPA, 24], F32)
        X16p = lambda name: sb0(name, [128, NPA, 16], F32)
        Br, Bi, c_re, c_im = X16p("S_Br"), X16p("S_Bi"), X16p("S_cre"), X16p("S_cim")
        lam_all = sb0("S_lamall", [128, 2, NPA], F32)
        thc_all = sb0("S_thc", [128, NPA], F32)
        rho_all = sb0("S_rho", [128, NPA], F32)
        mv = sb0("S_mv", [128, 128], F32)
        P.op("gpsimd", lambda e: e.iota(mv[:], pattern=[[1, 128]], base=1, channel_multiplier=0,
                                        allow_small_or_imprecise_dtypes=True), writes=[mv])
        with contextlib.ExitStack() as st1:
            sb1 = lambda name, shape, dt: Buf(st1.enter_context(nc.sbuf_tensor(_u(name), shape, dt)), name)
            V = lambda name: sb1(name, [128, NPA], F32)
            a_re, a_im, ldt, dt_, adr, adi = V("T_are"), V("T_aim"), V("T_ldt"), V("T_dt"), V("T_adr"), V("T_adi")
            xr, den, fr, fi, tv = V("T_xr"), V("T_den"), V("T_fr"), V("T_fi"), V("T_tv")
            W3 = lambda name, dt=F32: sb1(name, [128, NPA, 24], dt)
            argr, mag, ang, tsn, ff, dd, mm = W3("T_argr"), W3("T_mag"), W3("T_ang"), W3("T_tsn"), W3("T_ff"), W3("T_dd"), W3("T_mm")
            ii = W3("T_ii", I32)
            sinv, cosv = W3("T_sin"), W3("T_cos")
            X16 = lambda name: sb1(name, [128, NPA, 16], F32)
            b_re, b_im, x1, x2 = X16("T_bre"), X16("T_bim"), X16("T_x1"), X16("T_x2")
            P.dma("sync", "ld", a_re[:], inp["s5_aT_re"][j, :, :], writes=[a_re])
            P.dma("sync", "ld", a_im[:], inp["s5_aT_im"][j, :, :], writes=[a_im])
            P.dma("sync", "ld", ldt[:], inp["s5_ldt"][j, :, :], writes=[ldt])
            P.dma("sync", "ld", b_re[:], inp["s5_bT_re"][j, :, :, :], writes=[b_re])
            P.dma("sync", "ld", b_im[:], inp["s5_bT_im"][j, :, :, :], writes=[b_im])
            P.dma("sync", "ld", c_re[:], inp["s5_cT_re"][j, :, :, :], writes=[c_re])
            P.dma("sync", "ld", c_im[:], inp["s5_cT_im"][j, :, :, :], writes=[c_im])
            def vtt(out, a, bq, op, reads, writes, eng="vector"):
                P.op(eng, lambda e: e.tensor_tensor(out=out, in0=a, in1=bq, op=op), reads=reads, writes=writes)

            P.op("scalar", lambda e: e.activation(out=dt_[:], in_=ldt[:], func=AF.Exp), reads=[ldt], writes=[dt_])
            vtt(adr[:], a_re[:], dt_[:], ALU.mult, [a_re, dt_], [adr])
            vtt(adi[:], a_im[:], dt_[:], ALU.mult, [a_im, dt_], [adi])
            jvb = jv[:].unsqueeze(1).to_broadcast([128, NPA, 24])
            vtt(argr[:], adr[:].unsqueeze(2).to_broadcast([128, NPA, 24]), jvb, ALU.mult, [adr, jv], [argr])
            P.op("scalar", lambda e: e.activation(out=mag[:], in_=argr[:], func=AF.Exp), reads=[argr], writes=[mag])
            vtt(ang[:], adi[:].unsqueeze(2).to_broadcast([128, NPA, 24]), jvb, ALU.mult, [adi, jv], [ang])
            for off, dst in ((64.0, sinv), (64.25, cosv)):
                P.op("vector", lambda e: e.tensor_scalar(out=tsn[:], in0=ang[:], scalar1=1.0 / TWO_PI, scalar2=off,
                                                          op0=ALU.mult, op1=ALU.add), reads=[ang], writes=[tsn])
                P.op("vector", lambda e: e.tensor_copy(out=ii[:], in_=tsn[:]), reads=[tsn], writes=[ii])
                P.op("vector", lambda e: e.tensor_copy(out=ff[:], in_=ii[:]), reads=[ii], writes=[ff])
                vtt(dd[:], tsn[:], ff[:], ALU.subtract, [tsn, ff], [dd])
                P.op("vector", lambda e: e.tensor_single_scalar(out=mm[:], in_=dd[:], scalar=0.5, op=ALU.is_gt),
                     reads=[dd], writes=[mm])
                vtt(dd[:], dd[:], mm[:], ALU.subtract, [dd, mm], [dd])
                P.op("scalar", lambda e: e.activation(out=dst[:], in_=dd[:], func=AF.Sin, scale=SIN_SCALE),
                     reads=[dd], writes=[dst])
            vtt(Pr[:], mag[:], cosv[:], ALU.mult, [mag, cosv], [Pr])
            vtt(Pi[:], mag[:], sinv[:], ALU.mult, [mag, sinv], [Pi])
            P.op("vector", lambda e: e.tensor_scalar_add(out=xr[:], in0=Pr[:, :, 16], scalar1=-1.0), reads=[Pr], writes=[xr])
            vtt(den[:], a_re[:], a_re[:], ALU.mult, [a_re], [den])
            vtt(tv[:], a_im[:], a_im[:], ALU.mult, [a_im], [tv])
            vtt(den[:], den[:], tv[:], ALU.add, [den, tv], [den])
            P.op("vector", lambda e: e.reciprocal(out=den[:], in_=den[:]), reads=[den], writes=[den])
            vtt(fr[:], xr[:], a_re[:], ALU.mult, [xr, a_re], [fr])
            vtt(tv[:], Pi[:, :, 16], a_im[:], ALU.mult, [Pi, a_im], [tv])
            vtt(fr[:], fr[:], tv[:], ALU.add, [fr, tv], [fr])
            vtt(fr[:], fr[:], den[:], ALU.mult, [fr, den], [fr])
            vtt(fi[:], Pi[:, :, 16], a_re[:], ALU.mult, [Pi, a_re], [fi])
            vtt(tv[:], xr[:], a_im[:], ALU.mult, [xr, a_im], [tv])
            vtt(fi[:], fi[:], tv[:], ALU.subtract, [fi, tv], [fi])
            vtt(fi[:], fi[:], den[:], ALU.mult, [fi, den], [fi])
            frb = fr[:].unsqueeze(2).to_broadcast([128, NPA, 16])
            fib = fi[:].unsqueeze(2).to_broadcast([128, NPA, 16])
            vtt(x1[:], b_re[:], frb, ALU.mult, [b_re, fr], [x1])
            vtt(x2[:], b_im[:], fib, ALU.mult, [b_im, fi], [x2])
            vtt(Br[:], x1[:], x2[:], ALU.subtract, [x1, x2], [Br])
            vtt(x1[:], b_im[:], frb, ALU.mult, [b_im, fr], [x1])
            vtt(x2[:], b_re[:], fib, ALU.mult, [b_re, fi], [x2])
            vtt(Bi[:], x1[:], x2[:], ALU.add, [x1, x2], [Bi])
            P.op("vector", lambda e: e.tensor_copy(out=lam_all[:, 0, :], in_=Pr[:, :, 23]), reads=[Pr], writes=[lam_all])
            P.op("vector", lambda e: e.tensor_copy(out=lam_all[:, 1, :], in_=Pi[:, :, 23]), reads=[Pi], writes=[lam_all])
            P.op("vector", lambda e: e.tensor_copy(out=rho_all[:], in_=mag[:, :, 23]), reads=[mag], writes=[rho_all])
            P.op("vector", lambda e: e.tensor_scalar(out=tv[:], in0=adi[:], scalar1=8.0 / TWO_PI, scalar2=None, op0=ALU.mult),
                 reads=[adi], writes=[tv])
            P.op("vector", lambda e: e.tensor_copy(out=ii[:, :, 0], in_=tv[:]), reads=[tv], writes=[ii])
            P.op("vector", lambda e: e.tensor_copy(out=xr[:], in_=ii[:, :, 0]), reads=[ii], writes=[xr])
            vtt(thc_all[:], tv[:], xr[:], ALU.subtract, [tv, xr], [thc_all])

            P.barrier(keep=[C.pref])

        def vtt(out, a, bq, op, reads, writes, eng="vector"):
            P.op(eng, lambda e: e.tensor_tensor(out=out, in0=a, in1=bq, op=op), reads=reads, writes=writes)

        for b in range(NB):
            p0 = b * NPB
            ps_ = slice(p0, p0 + NPB)
            with contextlib.ExitStack() as st1:
                sb1 = lambda name, shape, dt: Buf(st1.enter_context(nc.sbuf_tensor(_u(name), shape, dt)), name)
                t1 = [sb1("T_t1%d" % i, [128, NPB, 8, 16], F32) for i in range(2)]
                t2 = [sb1("T_t2%d" % i, [128, NPB, 8, 16], F32) for i in range(2)]
                BSreT = sb1("T_BSreT", [128, NPB, 8, 16], BF16)
                BSimT = sb1("T_BSimT", [128, NPB, 8, 16], BF16)
                CAmre = [sb1("T_CAmre%d" % hh, [128, NPB, 8, 16], BF16) for hh in range(2)]
                CAmim = [sb1("T_CAmim%d" % hh, [128, NPB, 8, 16], BF16) for hh in range(2)]
                P.op("vector", lambda e: e.tensor_copy(out=lam[:], in_=lam_all[:, :, ps_]), reads=[lam_all], writes=[lam])
                tcount = [0]

                def ctab(outb, j0, X, Y, mode):
                    if isinstance(outb, list):
                        ctab(outb[0], j0, X, Y, mode)
                        P.op("scalar", lambda e: e.copy(out=outb[1][64:128], in_=outb[0][64:128]), reads=[outb[0]], writes=[outb[1]])
                        P.op("gpsimd", lambda e: e.memset(outb[1][0:64], 0.0), writes=[outb[1]])
                        P.op("gpsimd", lambda e: e.memset(outb[0][64:128], 0.0), reads=[outb[0]], writes=[outb[0]])
                        return
                    k = tcount[0] % 2
                    tcount[0] += 1
                    A, Bq = (X, Y) if mode == "re" else (Y, X)
                    sh = [128, NPB, 8, 16]
                    prb = Pr[:, ps_, j0:j0 + 8].unsqueeze(3).to_broadcast(sh)
                    pib = Pi[:, ps_, j0:j0 + 8].unsqueeze(3).to_broadcast(sh)
                    vtt(t1[k][:], prb, A[:, ps_].unsqueeze(2).to_broadcast(sh), ALU.mult, [Pr, A], [t1[k]])
                    vtt(t2[k][:], pib, Bq[:, ps_].unsqueeze(2).to_broadcast(sh), ALU.mult, [Pi, Bq], [t2[k]])
                    if mode == "re":
                        vtt(outb[:], t1[k][:], t2[k][:], ALU.subtract, [t1[k], t2[k]], [outb])
                    elif mode == "im":
                        vtt(outb[:], t1[k][:], t2[k][:], ALU.add, [t1[k], t2[k]], [outb])
                    else:
                        P.op("vector", lambda e: e.scalar_tensor_tensor(out=outb[:], in0=t1[k][:], scalar=-1.0, in1=t2[k][:],
                                                                         op0=ALU.mult, op1=ALU.subtract),
                             reads=[t1[k], t2[k]], writes=[outb])

                STOP = 99
                if STOP <= 1:
                    P.barrier(keep=[C.pref])
                    return
                ctab(BSreT, 0, Br, Bi, "re")
                ctab(BSimT, 0, Br, Bi, "im")
                ctab(CAmre, 8, c_re, c_im, "re")
                ctab(CAmim, 8, c_re, c_im, "nim")
                ctab(CAre, 16, c_re, c_im, "re")
                ctab(CAim, 16, c_re, c_im, "nim")

                if STOP <= 2:
                    P.barrier(keep=[C.pref])
                    return
                for jl0 in range(0, NPB, 4):
                    tp = C.tp[C.tp_i % 2]
                    C.tp_i += 1
                    first = True
                    for q in range(4):
                        jl = jl0 + q
                        for ri, tab in ((0, BSreT), (1, BSimT)):
                            P.op("tensor", lambda e: e.transpose(
                                out=tp[:, (q * 2 + ri) * 128:(q * 2 + ri + 1) * 128],
                                in_=tab[:, jl].rearrange("p s c -> p (s c)"), identity=C.ident[:]),
                                reads=[BSreT, BSimT, C.ident] if first else [], writes=[tp] if first else [], tick=False)
                            first = False
                    P.commit("tensor", reads=[BSreT, BSimT, C.ident], writes=[tp])
                    copy_op(C, evac_engine(C), BS[:, jl0:jl0 + 4].rearrange("p g r c -> p (g r c)"), tp[:, :],
                            reads=[tp], writes=[BS])
                for gl0 in range(0, 32, 4):
                    acc = next_acc(C)
                    first = True
                    for q in range(4):
                        gl = gl0 + q
                        jl, hh = gl // 2, gl % 2
                        fl = lambda t: t[:, jl].rearrange("p s c -> p (s c)")
                        P.op("tensor", lambda e: e.matmul(acc[:, q * 128:(q + 1) * 128], lhsT=fl(BSreT), rhs=fl(CAmre[hh]),
                                                          start=True, stop=False),
                             reads=[BSreT, BSimT] + CAmre + CAmim if first else [], writes=[acc] if first else [], tick=False)
                        first = False
                        P.op("tensor", lambda e: e.matmul(acc[:, q * 128:(q + 1) * 128], lhsT=fl(BSimT), rhs=fl(CAmim[hh]),
                                                          start=False, stop=True), tick=False)
                    P.commit("tensor", reads=[BSreT, BSimT] + CAmre + CAmim, writes=[acc])
                    P.op("vector", lambda e: e.tensor_tensor(
                        out=K0[:, gl0:gl0 + 4, :], in0=acc[:, :].rearrange("p (g f) -> p g f", f=128),
                        in1=mask01[:].unsqueeze(1).to_broadcast([128, 4, 128]), op=ALU.mult),
                        reads=[acc, mask01], writes=[K0])
                P.barrier(keep=[C.pref])

            if STOP <= 3:
                return
            with contextlib.ExitStack() as st2:
                sb2 = lambda name, shape, dt: Buf(st2.enter_context(nc.sbuf_tensor(_u(name), shape, dt)), name)
                Uk = sb2("M_Uk", [128, 8, 512], BF16)
                Uk2 = sb2("M_Uk2", [128, 32, 8, 16], BF16)
                UT = sb2("M_UT", [128, 32, 128], BF16)
                Sri = sb2("M_S", [128, 2, NPB, 128], F32)
                Hp = sb2("M_Hp", [128, 2, NPB, 128], BF16)
                T1 = sb2("M_T1", [128, NPB, 128], F32)
                T2 = sb2("M_T2", [128, NPB, 128], F32)
                Rc = sb2("M_Rc", [128, NPB, 128], F32)
                Rs = sb2("M_Rs", [128, NPB, 128], F32)
                iiR = sb2("M_iiR", [128, NPB, 128], I32)
                Hc = sb2("M_Hc", [128, 2, NPB], F32)
                tmp = [sb2("M_tmp%d" % i, [128, 8, 64], F32) for i in range(2)]
                sh3 = [128, NPB, 128]

                def vt(out, a, bq, op, reads, writes):
                    P.op("vector", lambda e: e.tensor_tensor(out=out, in0=a, in1=bq, op=op), reads=reads, writes=writes)

                vt(T1[:], thc_all[:, ps_].unsqueeze(2).to_broadcast(sh3), mv[:].unsqueeze(1).to_broadcast(sh3), ALU.mult,
                   [thc_all, mv], [T1])
                for off, dst in ((0.0, Rs), (0.25, Rc)):
                    if off:
                        P.op("vector", lambda e: e.tensor_scalar_add(out=T1[:], in0=T1[:], scalar1=off), reads=[T1], writes=[T1])
                    P.op("vector", lambda e: e.tensor_copy(out=iiR[:], in_=T1[:]), reads=[T1], writes=[iiR])
                    P.op("vector", lambda e: e.tensor_copy(out=T2[:], in_=iiR[:]), reads=[iiR], writes=[T2])
                    vt(T2[:], T1[:], T2[:], ALU.subtract, [T1, T2], [T2])
                    P.op("scalar", lambda e: e.activation(out=dst[:], in_=T2[:], func=AF.Sin, scale=SIN_SCALE),
                         reads=[T2], writes=[dst])
                P.op("gpsimd", lambda e: e.memset(Hc[:], 0.0), writes=[Hc])

                ti = 0
                for ct in range(NCT):
                    P.dma("sync", "ld", Uk[:],
                          proj[ct * 1024:(ct + 1) * 1024, b * 512:(b + 1) * 512].rearrange("(k s) c -> k s c", s=8),
                          reads=[proj], writes=[Uk])
                    P.op("scalar", lambda e: e.copy(out=Uk2[:], in_=Uk[:].rearrange("k s (g c) -> k g s c", c=16)),
                         reads=[Uk], writes=[Uk2])
                    for gl0 in range(0, 32, 8):
                        tp = C.tp[C.tp_i % 2]
                        C.tp_i += 1
                        for q in range(8):
                            gl = gl0 + q
                            P.op("tensor", lambda e: e.transpose(out=tp[:, q * 128:(q + 1) * 128],
                                                                 in_=Uk2[:, gl].rearrange("k s c -> k (s c)"),
                                                                 identity=C.ident[:]),
                                 reads=[Uk2, C.ident] if q == 0 else [], writes=[tp] if q == 0 else [], tick=False)
                        P.commit("tensor", reads=[Uk2, C.ident], writes=[tp])
                        copy_op(C, "scalar", UT[:, gl0:gl0 + 8, :],
                                tp[:, :].rearrange("p (a b) -> p a b", b=128), reads=[tp], writes=[UT])
                    for jl0 in range(0, NPB, 2):
                        acc = next_acc(C)
                        first = True
                        for q in range(2):
                            jl = jl0 + q
                            for ri in range(2):
                                slot = q * 2 + ri
                                for hh in range(2):
                                    gl = 2 * jl + hh
                                    P.op("tensor", lambda e: e.matmul(
                                        acc[hh * 64:(hh + 1) * 64, slot * 128:(slot + 1) * 128],
                                        lhsT=BS[:, jl, ri, hh * 64:(hh + 1) * 64], rhs=UT[:, gl, :], start=True, stop=True),
                                        reads=[BS, UT] if first else [], writes=[acc] if first else [], tick=False)
                                    first = False
                        P.commit("tensor", reads=[BS, UT], writes=[acc])
                        copy_op(C, "scalar", Sri[:, :, jl0:jl0 + 2, :].rearrange("p r q k -> p q r k"),
                                acc[:, :].rearrange("p (q r k) -> p q r k", q=2, r=2), reads=[acc], writes=[Sri])
                    Sr, Si = Sri[:, 0], Sri[:, 1]
                    vt(T1[:], Rc[:], Sr, ALU.mult, [Rc, Sri], [T1])
                    vt(T2[:], Rs[:], Si, ALU.mult, [Rs, Sri], [T2])
                    vt(T1[:], T1[:], T2[:], ALU.add, [T1, T2], [T1])
                    vt(T2[:], Rc[:], Si, ALU.mult, [Rc, Sri], [T2])
                    vt(Sr, Rs[:], Sr, ALU.mult, [Rs, Sri], [Sri])
                    vt(T2[:], T2[:], Sr, ALU.subtract, [T2, Sri], [T2])
                    P.op("vector", lambda e: e.tensor_copy(out=Hp[:, :, :, 0:1], in_=Hc[:].unsqueeze(3)),
                         reads=[Hc], writes=[Hp])
                    for bq in (T1, T2, Sri, rho_all, Hc):
                        P._pre("vector", [bq], [])
                    P._pre("vector", [], [Sri])
                    lastins = None
                    for jl in range(NPB):
                        rb = rho_all[:, p0 + jl:p0 + jl + 1].to_broadcast([128, 128])
                        nc.vector.tensor_tensor_scan(out=Sri[:, 0, jl, :], data0=rb, data1=T1[:, jl, :],
                                                     initial=Hc[:, 0, jl:jl + 1], op0=ALU.mult, op1=ALU.add)
                        lastins = nc.vector.tensor_tensor_scan(out=Sri[:, 1, jl, :], data0=rb, data1=T2[:, jl, :],
                                                               initial=Hc[:, 1, jl:jl + 1], op0=ALU.mult, op1=ALU.add)
                    P.last["vector"] = lastins
                    P.commit("vector", reads=[T1, T2, rho_all, Hc], writes=[Sri])
                    vt(T1[:], Rc[:], Sr, ALU.mult, [Rc, Sri], [T1])
                    vt(T2[:], Rs[:], Si, ALU.mult, [Rs, Sri], [T2])
                    vt(Hp[:, 0, :, 1:128], T1[:, :, 0:127], T2[:, :, 0:127], ALU.subtract, [T1, T2], [Hp])
                    vt(Hc[:, 0, :], T1[:, :, 127], T2[:, :, 127], ALU.subtract, [T1, T2], [Hc])
                    vt(T1[:], Rc[:], Si, ALU.mult, [Rc, Sri], [T1])
                    vt(T2[:], Rs[:], Sr, ALU.mult, [Rs, Sri], [T2])
                    vt(Hp[:, 1, :, 1:128], T1[:, :, 0:127], T2[:, :, 0:127], ALU.add, [T1, T2], [Hp])
                    vt(Hc[:, 1, :], T1[:, :, 127], T2[:, :, 127], ALU.add, [T1, T2], [Hc])
                    for gl0 in range(0, 32, 4):
                        acc = next_acc(C)
                        first = True
                        for q in range(4):
                            gl = gl0 + q
                            jl, hh = gl // 2, gl % 2
                            o = acc[:, q * 128:(q + 1) * 128]
                            P.op("tensor", lambda e: e.matmul(o, lhsT=UT[:, gl, :], rhs=K0[:, gl, :], start=True, stop=False),
                                 reads=[UT, K0, Hp] + CAre + CAim if first else [], writes=[acc] if first else [], tick=False)
                            first = False
                            P.op("tensor", lambda e: e.matmul(o, lhsT=Hp[:, 0, jl, :],
                                                              rhs=CAre[hh][:, jl].rearrange("p t c -> p (t c)"),
                                                              start=False, stop=False), tick=False)
                            P.op("tensor", lambda e: e.matmul(o, lhsT=Hp[:, 1, jl, :],
                                                              rhs=CAim[hh][:, jl].rearrange("p t c -> p (t c)"),
                                                              start=False, stop=True), tick=False)
                        P.commit("tensor", reads=[UT, K0, Hp] + CAre + CAim, writes=[acc])
                        ch0 = gl0 * 16
                        t = tmp[ti % 2]
                        ti += 1
                        P.op("vector", lambda e: e.tensor_tensor(
                            out=t[:], in0=Uk[:, :, ch0:ch0 + 64],
                            in1=C.bc2[:, b * 512 + ch0: b * 512 + ch0 + 64].unsqueeze(1).to_broadcast([128, 8, 64]),
                            op=ALU.mult), reads=[Uk, C.bc2], writes=[t])
                        P.op("vector", lambda e: e.tensor_tensor(
                            out=t[:].rearrange("p t (g c) -> p t g c", c=16), in0=t[:].rearrange("p t (g c) -> p t g c", c=16),
                            in1=acc[:, :].rearrange("p (g t c) -> p t g c", g=4, t=8, c=16), op=ALU.add),
                            reads=[t, acc], writes=[t])
                        P.op("scalar", lambda e: e.activation(out=Uk[:, :, ch0:ch0 + 64], in_=t[:],
                                                              func=AF.Gelu_apprx_tanh), reads=[t], writes=[Uk])
                    P.dma("sync", "st",
                          gbuf[ct * 1024:(ct + 1) * 1024, b * 512:(b + 1) * 512].rearrange("(k s) c -> k s c", s=8),
                          Uk[:], reads=[Uk], writes=[gbuf])
                P.barrier(keep=[C.pref])


def s5_glu(C, j, inp, gbuf, ybuf, nxt=None):
    P, nc, NTT = C.P, C.nc, C.NTT
    with contextlib.ExitStack() as st:
        sb = lambda name, shape, dt: Buf(st.enter_context(nc.sbuf_tensor(_u(name), shape, dt)), name)
        alloc_linear(C, st)
        gt = [sb("G_g%d" % i, [128, E], BF16) for i in range(4)]
        gs = [sb("G_gs%d" % i, [128, 512], BF16) for i in range(4)]
        tf = [sb("G_t%d" % i, [128, 512], F32) for i in range(4)]
        ys = [sb("G_y%d" % i, [128, 512], BF16) for i in range(4)]
        P.dma("sync", "ld", C.bc[:], _bcast_rows(inp["s5_b_glu"][j:j + 1, :], 128), writes=[C.bc])
        def produce(i):
            g = gt[i % 4]
            P.dma("gpsimd", "ld2", g[:], gbuf[i * 128:(i + 1) * 128, :], reads=[gbuf], writes=[g])
            transpose_tile(C, g, C.actT_t[i], i * 128)
        cnt = [0]

        def epi(i, c0, cw, acc):
            k = cnt[0] % 4
            cnt[0] += 1
            P.dma("gpsimd", "ld2", gs[k][:, 0:cw], gbuf[i * 128:(i + 1) * 128, c0:c0 + cw], reads=[gbuf], writes=[gs[k]])
            P.op("vector", lambda e: e.tensor_tensor(out=tf[k][:, 0:cw], in0=acc[:, 0:cw], in1=C.bc[:, c0:c0 + cw], op=ALU.add),
                 reads=[acc, C.bc], writes=[tf[k]])
            P.op("scalar", lambda e: e.activation(out=tf[k][:, 0:cw], in_=tf[k][:, 0:cw], func=AF.Sigmoid),
                 reads=[tf[k]], writes=[tf[k]])
            P.op("vector", lambda e: e.tensor_tensor(out=ys[k][:, 0:cw], in0=tf[k][:, 0:cw], in1=gs[k][:, 0:cw], op=ALU.mult),
                 reads=[tf[k], gs[k]], writes=[ys[k]])
            P.dma("sync", "st", ybuf[i * 128:(i + 1) * 128, c0:c0 + cw], ys[k][:, 0:cw], reads=[ys[k]], writes=[ybuf])

        linear(C, inp["s5_w_glu"][j], E, epi, produce=produce, nxt=nxt)
        P.barrier(keep=[C.pref])


def fox_mixer(C, inp, proj, fbuf, ybuf, seq_start):
    P, nc, NT, NTT = C.P, C.nc, C.NT, C.NTT
    scale = 128.0 ** -0.5
    with contextlib.ExitStack() as st:
        sb = lambda name, shape, dt: Buf(st.enter_context(nc.sbuf_tensor(_u(name), shape, dt)), name)
        qw, kw, tri = sb("F_qw", [128, 128], F32), sb("F_kw", [128, 128], F32), sb("F_tri", [128, 128], F32)
        negm = sb("F_negm", [128, 128], BF16)
        AT = sb("F_AT", [16, NT], F32)
        tmpf = sb("F_tmpf", [16, NT], F32)
        carry = sb("F_carry", [16, 2], F32)
        Ahi, Alo, NAhi, NAlo = [sb("F_A%d" % i, [16, NT], BF16) for i in range(4)]
        ft = [sb("F_f%d" % i, [128, 16], F32) for i in range(2)]
        KAz = sb("F_KAz", [128, 4, NT], BF16)
        QAz = sb("F_QAz", [128, 4, NT], BF16)
        qT = sb("F_qT", [128, 4, NT], BF16)
        kT = sb("F_kT", [128, 4, NT], BF16)
        vext = sb("F_v", [128, NTT, 4, 132], BF16)
        xt = [sb("F_x%d" % i, [128, 512], BF16) for i in range(4)]
        sq = [sb("F_sq%d" % i, [128, 512], F32) for i in range(2)]
        xn = [sb("F_xn%d" % i, [128, 512], BF16) for i in range(2)]
        ss = [sb("F_ss%d" % i, [128, 4], F32) for i in range(2)]
        PT = [sb("F_PT%d" % i, [128, 512], BF16) for i in range(3)]
        ystage = [sb("F_ys%d" % i, [128, 4, 128], BF16) for i in range(2)]
        rden = [sb("F_rd%d" % i, [128, 1], F32) for i in range(2)]

        P.dma("sync", "ld", qw[:], _bcast_rows(inp["fox_q_norm"][0:1, :], 128), writes=[qw])
        P.dma("sync", "ld", kw[:], _bcast_rows(inp["fox_k_norm"][0:1, :], 128), writes=[kw])
        P.op("vector", lambda e: e.tensor_scalar(out=qw[:], in0=qw[:], scalar1=scale, scalar2=None, op0=ALU.mult),
             reads=[qw], writes=[qw])
        P.op("gpsimd", lambda e: e.memset(tri[:], 1.0), writes=[tri])
        P.op("gpsimd", lambda e: e.affine_select(out=tri[:], in_=tri[:], pattern=[[1, 128]], compare_op=ALU.is_ge,
                                                  fill=0.0, base=0, channel_multiplier=-1), reads=[tri], writes=[tri])
        P.op("gpsimd", lambda e: e.memset(negm[:], 0.0), writes=[negm])
        P.op("gpsimd", lambda e: e.affine_select(out=negm[:], in_=negm[:], pattern=[[1, 128]], compare_op=ALU.is_ge,
                                                  fill=-30000.0, base=0, channel_multiplier=-1), reads=[negm], writes=[negm])
        P.op("gpsimd", lambda e: e.memset(carry[:], 0.0), writes=[carry])
        P.op("gpsimd", lambda e: e.memset(KAz[:], 0.0), writes=[KAz])
        P.op("gpsimd", lambda e: e.memset(KAz[0:2], 1.0), reads=[KAz], writes=[KAz])
        P.op("gpsimd", lambda e: e.memset(QAz[:], 1.0), writes=[QAz])
        P.op("gpsimd", lambda e: e.memset(vext[:], 1.0), writes=[vext])

        for i in range(NTT):
            f = ft[i % 2]
            P.dma("sync", "ld", f[:], fbuf[i * 128:(i + 1) * 128, :], reads=[fbuf], writes=[f])
            P.op("scalar", lambda e: e.activation(out=f[:], in_=f[:], func=AF.Exp, scale=-1.0), reads=[f], writes=[f])
            P.op("vector", lambda e: e.tensor_scalar_add(out=f[:], in0=f[:], scalar1=1.0), reads=[f], writes=[f])
            P.op("scalar", lambda e: e.activation(out=f[:], in_=f[:], func=AF.Ln), reads=[f], writes=[f])
            acc = C.acc[i % 4]
            P.op("tensor", lambda e: e.matmul(acc[0:16, 0:128], lhsT=f[:, :], rhs=tri[:, :], start=True, stop=True),
                 reads=[f, tri], writes=[acc])
            P.op("vector", lambda e: e.tensor_scalar(out=AT[:, i * 128:(i + 1) * 128], in0=acc[0:16, 0:128],
                                                      scalar1=carry[:, 0:1], scalar2=None, op0=ALU.add),
                 reads=[acc, carry], writes=[AT])
            P.op("vector", lambda e: e.tensor_copy(out=carry[:, 0:1], in_=AT[:, i * 128 + 127:i * 128 + 128]),
                 reads=[AT], writes=[carry])
        P.op("vector", lambda e: e.tensor_copy(out=Ahi[:], in_=AT[:]), reads=[AT], writes=[Ahi])
        P.op("vector", lambda e: e.tensor_copy(out=tmpf[:], in_=Ahi[:]), reads=[Ahi], writes=[tmpf])
        P.op("vector", lambda e: e.tensor_tensor(out=tmpf[:], in0=AT[:], in1=tmpf[:], op=ALU.subtract),
             reads=[AT, tmpf], writes=[tmpf])
        P.op("vector", lambda e: e.tensor_copy(out=Alo[:], in_=tmpf[:]), reads=[tmpf], writes=[Alo])
        P.op("vector", lambda e: e.tensor_scalar(out=NAhi[:], in0=Ahi[:], scalar1=-1.0, scalar2=None, op0=ALU.mult),
             reads=[Ahi], writes=[NAhi])
        P.op("vector", lambda e: e.tensor_scalar(out=NAlo[:], in0=Alo[:], scalar1=-1.0, scalar2=None, op0=ALU.mult),
             reads=[Alo], writes=[NAlo])

        sidx, oidx, pidx, xi, ni = 0, 0, 0, 0, 0
        for hg in range(4):
            for hl in range(4):
                h = hg * 4 + hl
                P.dma("sync", "ld", KAz[2:3, hl, :], Ahi[h:h + 1, :], reads=[Ahi], writes=[KAz])
                P.dma("sync", "ld", KAz[3:4, hl, :], Alo[h:h + 1, :], reads=[Alo], writes=[KAz])
                P.dma("sync", "ld", QAz[0:1, hl, :], NAhi[h:h + 1, :], reads=[NAhi], writes=[QAz])
                P.dma("sync", "ld", QAz[1:2, hl, :], NAlo[h:h + 1, :], reads=[NAlo], writes=[QAz])
            for i in range(NTT):
                rows = slice(i * 128, (i + 1) * 128)
                P.dma("sync", "ld", vext[:, i, :, 0:128],
                      proj[rows, 2 * E + hg * 512: 2 * E + (hg + 1) * 512].rearrange("t (h d) -> t h d", d=128),
                      reads=[proj], writes=[vext])
                for col0, dstT in ((0, qT), (E, kT)):
                    x = xt[xi % 4]
                    xi += 1
                    P.dma("sync", "ld", x[:], proj[rows, col0 + hg * 512: col0 + (hg + 1) * 512], reads=[proj], writes=[x])
                    transpose_tile(C, x, dstT, i * 128, nchunks=4)

            items = []
            for i in range(NTT):
                for hl in range(4):
                    gl = [list(range(j0, min(j0 + 4, i + 1))) for j0 in range(0, i + 1, 4)]
                    for gi, grp in enumerate(gl):
                        items.append((i, hl, grp, gi == len(gl) - 1))

            def emit_S(it):
                nonlocal sidx
                i, hl, grp, _ = it
                qs = slice(i * 128, (i + 1) * 128)
                sacc = C.acc[sidx % 4]
                sidx += 1
                first = True
                for jj, j in enumerate(grp):
                    o = sacc[:, jj * 128:(jj + 1) * 128]
                    ksl = slice(j * 128, (j + 1) * 128)
                    P.op("tensor", lambda e: e.matmul(o, lhsT=kT[:, hl, ksl], rhs=qT[:, hl, qs], start=True, stop=False),
                         reads=[kT, qT, KAz, QAz, negm, C.ident] if first else [], writes=[sacc] if first else [],
                         tick=False)
                    first = False
                    P.op("tensor", lambda e: e.matmul(o, lhsT=KAz[:, hl, ksl], rhs=QAz[:, hl, qs], start=False,
                                                      stop=(j != i)), tick=False)
                    if j == i:
                        P.op("tensor", lambda e: e.matmul(o, lhsT=C.ident[:], rhs=negm[:], start=False, stop=True),
                             tick=False)
                P.commit("tensor", reads=[kT, qT, KAz, QAz, negm, C.ident], writes=[sacc])
                return sacc

            def emit_PV(it, sacc):
                nonlocal pidx
                i, hl, grp, last = it
                qs = slice(i * 128, (i + 1) * 128)
                oacc = C.acc[4 + ((i * 4 + hl) % 2)]
                ys = ystage[i % 2]
                pt = PT[pidx % 3]
                pidx += 1
                n = len(grp)
                P.op("scalar", lambda e: e.activation(out=pt[:, 0:n * 128], in_=sacc[:, 0:n * 128], func=AF.Exp),
                     reads=[sacc], writes=[pt])
                for jj, j in enumerate(grp):
                    P.op("tensor", lambda e: e.matmul(oacc[:, 0:129], lhsT=pt[:, jj * 128:(jj + 1) * 128],
                                                      rhs=vext[:, j, hl, 0:129], start=(j == 0), stop=(j == i)),
                         reads=[pt, vext] if jj == 0 else [], writes=[oacc] if j == 0 else [], tick=False)
                if not last:
                    P.commit("tensor", reads=[pt, vext])
                    return
                P.commit("tensor", reads=[pt, vext], writes=[oacc])
                rd = rden[(i * 4 + hl) % 2]
                P.op("vector", lambda e: e.reciprocal(out=rd[:], in_=oacc[:, 128:129]), reads=[oacc], writes=[rd])
                P.op("vector", lambda e: e.tensor_scalar(out=ys[:, hl, :], in0=oacc[:, 0:128], scalar1=rd[:, 0:1],
                                                          scalar2=None, op0=ALU.mult), reads=[oacc, rd], writes=[ys])
                if hl == 3:
                    P.dma("sync", "st", ybuf[qs, hg * 512:(hg + 1) * 512], ys[:].rearrange("p h d -> p (h d)"),
                          reads=[ys], writes=[ybuf])

            LOOK = 3
            pend = []
            for it in items:
                pend.append((it, emit_S(it)))
                if len(pend) > LOOK:
                    emit_PV(*pend.pop(0))
            while pend:
                emit_PV(*pend.pop(0))
            P.barrier(keep=[C.pref])


def _s5_layouts(inputs):
    o = {}
    f = np.ascontiguousarray
    a_re, a_im, ldt = inputs["s5_a_re"], inputs["s5_a_im"], inputs["s5_log_dt"]
    n = a_re.shape[0]
    o["s5_aT_re"] = f(a_re.reshape(n, 64, 2, 64).transpose(0, 2, 3, 1).reshape(n, 128, 64))
    o["s5_aT_im"] = f(a_im.reshape(n, 64, 2, 64).transpose(0, 2, 3, 1).reshape(n, 128, 64))
    o["s5_ldt"] = f(np.broadcast_to(ldt.reshape(n, 64, 2, 1), (n, 64, 2, 64)).transpose(0, 2, 3, 1).reshape(n, 128, 64))
    for nm in ("b_re", "b_im"):
        b = inputs["s5_" + nm]
        o["s5_bT_" + nm[2:]] = f(b.reshape(n, 64, 2, 64, 16).transpose(0, 2, 3, 1, 4).reshape(n, 128, 64, 16))
    for nm in ("c_re", "c_im"):
        c = inputs["s5_" + nm]
        o["s5_cT_" + nm[2:]] = f(c.reshape(n, 64, 2, 16, 64).transpose(0, 2, 4, 1, 3).reshape(n, 128, 64, 16))
    return o


def make_in_map(inputs, xrows, kinds):
    m = {"x": np.ascontiguousarray(xrows), "norm_w": inputs["norm_w"], "out_proj": inputs["out_proj"]}
    if 0 in kinds:
        m.update(_s5_layouts(inputs))
        for nm in ("s5_in_proj", "s5_d", "s5_w_glu", "s5_b_glu"):
            m[nm] = inputs[nm]
    if 1 in kinds:
        for nm in ("fox_in_proj", "fox_q_norm", "fox_k_norm", "fox_f_bias"):
            m[nm] = inputs[nm]
    if 2 in kinds:
        for nm in ("pool_in_proj", "pool_w_group", "pool_scale"):
            m[nm] = inputs[nm]
    return {k: np.ascontiguousarray(v, dtype=np.float32) for k, v in m.items()}


_PROG_CACHE = {}


def run_layers(inputs, h, layers, ncores=NCORES):
    B, L, _ = h.shape
    key = (tuple(layers), L)
    if key not in _PROG_CACHE:
        _PROG_CACHE[key] = build_program(list(layers), L)
    nc = _PROG_CACHE[key]
    kinds = sorted(set(l % 3 for l in layers))
    in_maps = [make_in_map(inputs, h[c % B], kinds) for c in range(ncores)]
    res = run_bass_kernel_spmd(nc, in_maps, core_ids=list(range(ncores)))
    return np.stack([res.results[c]["out"] for c in range(B)], axis=0)


def kernel(**inputs):
    inputs = {k: np.asarray(v) for k, v in inputs.items()}
    h = inputs["x"].astype(np.float32, copy=False)
    h = run_layers(inputs, h, [0, 1, 2, 3])
    return h.astype(np.float32, copy=False)
```
